# Optimizing a Trainium2 kernel written in Bass

```python
import jax, jax.numpy as jnp
from jax import lax
import numpy as np

D_MODEL = 1024
BATCH = 8
SEQ = 2048
DEPTH = 2
DEC_BATCH = 4
DEC_SEQ = 8192
PAST_LEN = 128

W_CONV = D_MODEL
CONV_HEADS = 8
CONV_K = 3
W_POOL = D_MODEL
POOL_WINDOWS = (2, 4, 8, 16)
N_POOL = len(POOL_WINDOWS)
POOL_DH = W_POOL // N_POOL
W_SGU = D_MODEL
SGU_HEADS = 4
SGU_DH = W_SGU // SGU_HEADS
CHUNK = 128
W_FNET = D_MODEL
FNET_GROUPS = 4
FNET_DH = W_FNET // FNET_GROUPS

N_EVEN = (DEPTH + 1) // 2
N_ODD = DEPTH // 2
EVEN_SPLITS = tuple(int(s) for s in np.cumsum([W_CONV, W_CONV, W_CONV, W_CONV, W_POOL]))
EVEN_IN = 4 * W_CONV + 2 * W_POOL
ODD_SPLITS = tuple(int(s) for s in np.cumsum([W_SGU, W_SGU, W_SGU, W_FNET]))
ODD_IN = 3 * W_SGU + 2 * W_FNET
EPS = 1e-6

kernel_name = "hybrid_conv_pool_sgu_fnet_encoder"


def rmsnorm(x, g):
    x32 = x.astype(jnp.float32)
    y = x32 * lax.rsqrt(jnp.mean(x32 * x32, axis=-1, keepdims=True) + EPS)
    return (y * g.astype(jnp.float32)).astype(x.dtype)


def short_conv_mixer(h, gb, gc, w):
    u = gc * h
    s = u.shape[1]
    pad = CONV_K // 2
    up = jnp.pad(u, ((0, 0), (pad, pad), (0, 0)))
    y = up[:, 0:s] * w[0]
    for k in range(1, CONV_K):
        y = y + up[:, k:k + s] * w[k]
    return gb * y


def multiscale_pool_mixer(v, w_grp, scale):
    bsz, s, _ = v.shape
    v32 = v.astype(jnp.float32)
    cs = jnp.concatenate([jnp.zeros((bsz, 1, W_POOL), jnp.float32), jnp.cumsum(v32, axis=1)], axis=1)
    t = np.arange(s)
    outs = []
    for g, win in enumerate(POOL_WINDOWS):
        lo = np.clip(t - win // 2, 0, s)
        hi = np.clip(t + win - win // 2, 0, s)
        cnt = (hi - lo).astype(np.float32)
        csg = cs[:, :, g * POOL_DH:(g + 1) * POOL_DH]
        mean = (csg[:, hi] - csg[:, lo]) / cnt[None, :, None]
        outs.append(mean - v32[:, :, g * POOL_DH:(g + 1) * POOL_DH])
    d = jnp.stack(outs, axis=2)
    y = jnp.einsum('bsgc,gcd->bsgd', d, w_grp.astype(jnp.float32)).reshape(bsz, s, W_POOL)
    return (y * scale.astype(jnp.float32)).astype(v.dtype)


def chunked_sgu(u, v, g_norm, w_s, b_s):
    bsz, s, _ = v.shape
    v32 = v.astype(jnp.float32).reshape(bsz, s // CHUNK, CHUNK, SGU_HEADS, SGU_DH)
    mu = jnp.mean(v32, axis=-1, keepdims=True)
    var = jnp.mean(jnp.square(v32 - mu), axis=-1, keepdims=True)
    vn = (v32 - mu) * lax.rsqrt(var + EPS) * g_norm.astype(jnp.float32).reshape(SGU_HEADS, SGU_DH)
    mix = jnp.einsum('hqp,bnphd->bnqhd', w_s.astype(jnp.float32), vn)
    mix = mix + jnp.transpose(b_s.astype(jnp.float32))[None, None, :, :, None]
    return u * mix.reshape(bsz, s, W_SGU).astype(u.dtype)


def fourier_mixer(f, w_grp):
    bsz, s, _ = f.shape
    f32 = f.astype(jnp.float32).reshape(bsz, s, FNET_GROUPS, FNET_DH)
    spec = jnp.real(jnp.fft.fft2(f32, axes=(1, 3), norm='ortho')).astype(jnp.float32)
    y = jnp.einsum('bsgc,gcd->bsgd', spec, w_grp.astype(jnp.float32))
    return y.reshape(bsz, s, W_FNET).astype(f.dtype)


def even_layer(x, g, w_in, conv_w, pool_w, pool_scale, w_out):
    h = rmsnorm(x, g)
    p = jnp.einsum('bsd,de->bse', h, w_in)
    a_h, a_b, a_c, a_z, b_v, b_z = jnp.split(p, EVEN_SPLITS, axis=-1)
    y_a = short_conv_mixer(a_h, a_b, a_c, conv_w) * jax.nn.silu(a_z)
    y_b = multiscale_pool_mixer(b_v, pool_w, pool_scale) * jax.nn.silu(b_z)
    y = jnp.concatenate([y_a, y_b], axis=-1)
    return x + jnp.einsum('bse,ed->bsd', y, w_out)


def odd_layer(x, g, w_in, sgu_norm_g, sgu_ws, sgu_bs, fnet_w, w_out):
    h = rmsnorm(x, g)
    p = jnp.einsum('bsd,de->bse', h, w_in)
    c_u, c_v, c_z, d_f, d_z = jnp.split(p, ODD_SPLITS, axis=-1)
    y_c = chunked_sgu(c_u, c_v, sgu_norm_g, sgu_ws, sgu_bs) * jax.nn.silu(c_z)
    y_d = fourier_mixer(d_f, fnet_w) * jax.nn.silu(d_z)
    y = jnp.concatenate([y_c, y_d], axis=-1)
    return x + jnp.einsum('bse,ed->bsd', y, w_out)


def trunk(x, norm_g, ev_w_in, ev_conv_w, ev_pool_w, ev_pool_scale, ev_w_out,
          od_w_in, od_sgu_norm_g, od_sgu_ws, od_sgu_bs, od_fnet_w, od_w_out, final_g):
    for i in range(DEPTH):
        j = i // 2
        if i % 2 == 0:
            x = even_layer(x, norm_g[i], ev_w_in[j], ev_conv_w[j], ev_pool_w[j], ev_pool_scale[j], ev_w_out[j])
        else:
            x = odd_layer(x, norm_g[i], od_w_in[j], od_sgu_norm_g[j], od_sgu_ws[j], od_sgu_bs[j], od_fnet_w[j], od_w_out[j])
    return rmsnorm(x, final_g)


def setup_inputs(seed: int = 0) -> dict:
    key = jax.random.key(seed)
    ks = jax.random.split(key, 16)
    f32 = jnp.float32
    nrm = lambda k, shp: jax.random.normal(k, shp, f32)
    return {
        "x_prompt": nrm(ks[0], (BATCH, SEQ, D_MODEL)),
        "x_sample": nrm(ks[1], (DEC_BATCH, DEC_SEQ, D_MODEL)),
        "norm_g": 1.0 + 0.05 * nrm(ks[2], (DEPTH, D_MODEL)),
        "ev_w_in": nrm(ks[3], (N_EVEN, D_MODEL, EVEN_IN)) * D_MODEL ** -0.5,
        "ev_conv_w": nrm(ks[4], (N_EVEN, CONV_K, W_CONV)) * CONV_K ** -0.5,
        "ev_pool_w": nrm(ks[5], (N_EVEN, N_POOL, POOL_DH, POOL_DH)) * POOL_DH ** -0.5,
        "ev_pool_scale": 1.0 + 0.1 * nrm(ks[6], (N_EVEN, W_POOL)),
        "ev_w_out": nrm(ks[7], (N_EVEN, W_CONV + W_POOL, D_MODEL)) * (0.5 * (W_CONV + W_POOL) ** -0.5),
        "od_w_in": nrm(ks[8], (N_ODD, D_MODEL, ODD_IN)) * D_MODEL ** -0.5,
        "od_sgu_norm_g": 1.0 + 0.05 * nrm(ks[9], (N_ODD, W_SGU)),
        "od_sgu_ws": nrm(ks[10], (N_ODD, SGU_HEADS, CHUNK, CHUNK)) * CHUNK ** -0.5,
        "od_sgu_bs": 0.02 * nrm(ks[11], (N_ODD, SGU_HEADS, CHUNK)),
        "od_fnet_w": nrm(ks[12], (N_ODD, FNET_GROUPS, FNET_DH, FNET_DH)) * FNET_DH ** -0.5,
        "od_w_out": nrm(ks[13], (N_ODD, W_SGU + W_FNET, D_MODEL)) * (0.5 * (W_SGU + W_FNET) ** -0.5),
        "final_g": 1.0 + 0.05 * nrm(ks[14], (D_MODEL,)),
    }


def reference(x_prompt, x_sample, norm_g, ev_w_in, ev_conv_w, ev_pool_w, ev_pool_scale, ev_w_out,
              od_w_in, od_sgu_norm_g, od_sgu_ws, od_sgu_bs, od_fnet_w, od_w_out, final_g):
    y_prompt = trunk(x_prompt, norm_g, ev_w_in, ev_conv_w, ev_pool_w, ev_pool_scale, ev_w_out,
                     od_w_in, od_sgu_norm_g, od_sgu_ws, od_sgu_bs, od_fnet_w, od_w_out, final_g)
    y_sample = trunk(x_sample, norm_g, ev_w_in, ev_conv_w, ev_pool_w, ev_pool_scale, ev_w_out,
                     od_w_in, od_sgu_norm_g, od_sgu_ws, od_sgu_bs, od_fnet_w, od_w_out, final_g)
    return (y_prompt, y_sample)
```

```python
import numpy as np
from contextlib import ExitStack
import concourse.bass as bass
import concourse.mybir as mybir
from concourse.bass_utils import run_bass_kernel_spmd

F32 = mybir.dt.float32
BF16 = mybir.dt.bfloat16
I32 = mybir.dt.int32
ALU = mybir.AluOpType
AF = mybir.ActivationFunctionType
D = 1024
EPS = 1e-6


class Obj:
    __slots__ = ("name", "w", "r")

    def __init__(self, name):
        self.name = name
        self.w = None
        self.r = {}


class Sched:
    COMPUTE = ("pe", "act", "dve", "pool")

    def __init__(self, nc, stack):
        self.nc = nc
        self.stack = stack
        self.q = {e: [] for e in ("pe", "act", "dve", "pool", "sp")}
        self.semh = {}
        self.cnt = {}
        self.waited = {e: {} for e in self.q}
        for e in self.COMPUTE:
            self._mksem("c_" + e)
        self.ninst = {e: 0 for e in self.q}

    def _mksem(self, key):
        h = self.stack.enter_context(self.nc.semaphore(key))
        self.semh[key] = h
        self.cnt[key] = 0
        return h

    def dsem(self, key):
        if key not in self.semh:
            self._mksem(key)
        return key

    def _deps(self, eng, reads, writes):
        deps = {}

        def add(k, v):
            if eng == "pe" and k == "c_pe":
                return
            if deps.get(k, 0) < v:
                deps[k] = v
        for o in reads:
            if o.w is not None:
                add(*o.w)
        for o in writes:
            if o.w is not None:
                add(*o.w)
            for k_, v_ in o.r.items():
                add(k_, v_)
        wl = []
        wd = self.waited[eng]
        for k, v in deps.items():
            if wd.get(k, 0) < v:
                wd[k] = v
                wl.append((k, v))
        return wl

    def emit(self, eng, fn, reads=(), writes=(), inc=True, dsem=None):
        wl = self._deps(eng, reads, writes)
        if dsem is not None:
            key = self.dsem(dsem)
            self.cnt[key] += 16
            tok = (key, self.cnt[key])
            incspec = (key, 16)
        else:
            key = "c_" + eng
            if inc:
                self.cnt[key] += 1
                tok = (key, self.cnt[key])
                incspec = (key, 1)
            else:
                tok = (key, self.cnt[key] + 1)
                incspec = None
        self.q[eng].append((wl, fn, incspec))
        self.ninst[eng] += 1
        for o in reads:
            if o.r.get(tok[0], 0) < tok[1]:
                o.r[tok[0]] = tok[1]
        for o in writes:
            o.w = tok
            o.r = {}
        return tok

    def barrier(self):
        for eng in self.q:
            wl = []
            wd = self.waited[eng]
            for k, v in self.cnt.items():
                if v > 0 and wd.get(k, 0) < v:
                    wd[k] = v
                    wl.append((k, v))
            if wl:
                self.q[eng].append((wl, None, None))

    def replay(self):
        nc = self.nc
        semh = self.semh
        q = self.q

        def run(engobj, lst):
            for wl, fn, incspec in lst:
                for k, v in wl:
                    engobj.wait_ge(semh[k], v)
                if fn is None:
                    continue
                ins = fn(engobj)
                if incspec is not None:
                    ins.then_inc(semh[incspec[0]], incspec[1])

        with nc.Block() as block:
            @block.tensor
            def _(e):
                run(e, q["pe"])

            @block.scalar
            def _(e):
                run(e, q["act"])

            @block.vector
            def _(e):
                run(e, q["dve"])

            @block.gpsimd
            def _(e):
                run(e, q["pool"])

            @block.sync
            def _(e):
                run(e, q["sp"])
        for e in q:
            q[e] = []


def rsqrt_dve(S, y, x, tmp, Oy, Ox, Otmp, iters=3):
    xi = x.bitcast(I32)
    yi = y.bitcast(I32)
    S.emit("dve", lambda e: e.tensor_scalar(out=yi, in0=xi, scalar1=1, scalar2=None, op0=ALU.arith_shift_right),
           reads=[Ox], writes=[Oy])
    S.emit("dve", lambda e: e.tensor_scalar(out=yi, in0=yi, scalar1=-1, scalar2=0x5f3759df, op0=ALU.mult, op1=ALU.add),
           reads=[Oy], writes=[Oy])
    for _ in range(iters):
        S.emit("dve", lambda e: e.tensor_tensor(out=tmp, in0=y, in1=y, op=ALU.mult), reads=[Oy], writes=[Otmp])
        S.emit("dve", lambda e: e.tensor_tensor(out=tmp, in0=tmp, in1=x, op=ALU.mult), reads=[Otmp, Ox], writes=[Otmp])
        S.emit("dve", lambda e: e.tensor_scalar(out=tmp, in0=tmp, scalar1=-0.5, scalar2=1.5, op0=ALU.mult, op1=ALU.add),
               reads=[Otmp], writes=[Otmp])
        S.emit("dve", lambda e: e.tensor_tensor(out=y, in0=y, in1=tmp, op=ALU.mult), reads=[Oy, Otmp], writes=[Oy])


def build(NB, debug=False, stop_after=None, lim=99):
    NT = 128 * NB
    NU = NT // 512
    LB = NT // 4
    NB4 = NB // 4
    KP = 2 * NB
    KU = 512 // NB
    R1 = 128 // NB
    KPB = 512 // (2 * NB)
    NBG = max(1, NB // 16)
    BG = min(16, NB)

    nc = bass.Bass("TRN2", target_bir_lowering=False)

    def din(name, shape, dt=F32):
        return nc.dram_tensor(name, shape, dt, kind="ExternalInput").ap()

    def dscr(name, shape, dt):
        if debug and name in ("X1", "X1P", "F1", "F2", "G", "Tscr"):
            return nc.dram_tensor(name, shape, dt, kind="ExternalOutput").ap()
        return nc.dram_tensor(name, shape, dt).ap()

    x_d = din("x", [NT, D])
    w_in0 = din("w_in0", [D, 6144])
    w_out0 = din("w_out0", [2048, D])
    w_in1 = din("w_in1", [D, 5120])
    w_out1 = din("w_out1", [2048, D])
    g0_d = din("g0", [D])
    g1_d = din("g1", [D])
    gf_d = din("gf", [D])
    convw_d = din("convw_l", [128, 24])
    pscale_d = din("pscale_l", [128, 8])
    poolw_d = din("poolw_l", [128, 2048])
    wsT_d = din("wsT_l", [128, 512])
    sgug_d = din("sgug_l", [128, 8])
    bs_d = din("bs_l", [2048])
    fnetw_d = din("fnetw_l", [128, 2048])
    ident_d = din("ident", [128, 128])
    MA_d = din("MA", [128, 512])
    MC_d = din("MC", [KP, 128 * 2 * NB])
    DC_d = din("DC", [128, 1024])
    inv_d = din("inv_tab", [128, 128])
    beta_d = din("beta", [128, 1])
    y_d = nc.dram_tensor("y", [NT, D], F32, kind="ExternalOutput").ap()

    wb_in0 = dscr("wb_in0", [D, 6144], BF16)
    wb_out0 = dscr("wb_out0", [2048, D], BF16)
    wb_in1 = dscr("wb_in1", [D, 5120], BF16)
    wb_out1 = dscr("wb_out1", [2048, D], BF16)
    X1 = dscr("X1", [NT, D], F32)
    X1P = dscr("X1P", [NT, D], F32)
    F1 = dscr("F1", [NT, D], BF16)
    F2 = dscr("F2", [NT, D], BF16)
    G = dscr("G", [NT, D], BF16)
    Tscr = dscr("Tscr", [128, KP * D], BF16)

    with ExitStack() as st:
        S = Sched(nc, st)
        banks = [st.enter_context(nc.psum_tensor("bk%d" % i, [128, 512], F32)) for i in range(8)]
        Ob = [Obj("bk%d" % i) for i in range(8)]
        bstate = [0]

        def nb():
            i = bstate[0]
            bstate[0] = (i + 1) % 8
            return banks[i], Ob[i]

        def sbt(ph, name, shape, dt):
            return ph.enter_context(nc.sbuf_tensor("s_" + name, shape, dt))

        def end_phase():
            S.barrier()
            S.replay()

        with ExitStack() as ph:
            NBUF = 6
            wf = [sbt(ph, "wf%d" % i, [128, 2048], F32) for i in range(NBUF)]
            wbb = [sbt(ph, "wbb%d" % i, [128, 2048], BF16) for i in range(NBUF)]
            Owf = [Obj("wf%d" % i) for i in range(NBUF)]
            Owb = [Obj("wbb%d" % i) for i in range(NBUF)]
            k = 0
            for src, dst, R, C in ((w_in0, wb_in0, D, 6144), (w_out0, wb_out0, 2048, D)):
                for rb in range(R // 128):
                    for c0 in range(0, C, 2048):
                        cw = min(2048, C - c0)
                        b = k % NBUF
                        S.emit("sp", lambda e, b=b, src=src, rb=rb, c0=c0, cw=cw: e.dma_start(
                            out=wf[b][:, 0:cw], in_=src[rb * 128:(rb + 1) * 128, c0:c0 + cw]),
                            writes=[Owf[b]], dsem="wl%d" % b)
                        if k % 2 == 0:
                            S.emit("act", lambda e, b=b, cw=cw: e.activation(out=wbb[b][:, 0:cw], in_=wf[b][:, 0:cw], func=AF.Copy),
                                   reads=[Owf[b]], writes=[Owb[b]])
                        else:
                            S.emit("dve", lambda e, b=b, cw=cw: e.tensor_copy(out=wbb[b][:, 0:cw], in_=wf[b][:, 0:cw]),
                                   reads=[Owf[b]], writes=[Owb[b]])
                        S.emit("pool", lambda e, b=b, dst=dst, rb=rb, c0=c0, cw=cw: e.dma_start(
                            out=dst[rb * 128:(rb + 1) * 128, c0:c0 + cw], in_=wbb[b][:, 0:cw]),
                            reads=[Owb[b]], dsem="ws%d" % b)
                        k += 1
            end_phase()
        if stop_after == "W":
            return nc

        NRING = 5

        with ExitStack() as ph:
            NR0 = 6
            ring = [sbt(ph, "ring%d" % i, [128, 4096], BF16) for i in range(NR0)]
            Oring = [Obj("ring%d" % i) for i in range(NR0)]
            rstate = [0]

            def wload(src_ap_fn):
                i = rstate[0]
                rstate[0] = (i + 1) % NR0
                o, i_ap = src_ap_fn(ring[i])
                S.emit("sp", lambda e, o=o, i_ap=i_ap: e.dma_start(out=o, in_=i_ap), writes=[Oring[i]], dsem="wr%d" % i)
                return ring[i], Oring[i]

            g0r = sbt(ph, "g0r", [128, D], F32)
            convw = sbt(ph, "convw", [128, 24], F32)
            pscale = sbt(ph, "pscale", [128, 8], F32)
            poolw_f = ring[NR0 - 1]
            poolw = sbt(ph, "poolw", [128, 2048], BF16)
            ident_f = sbt(ph, "ident_f", [128, 128], F32)
            ident = sbt(ph, "ident", [128, 128], BF16)
            invt = sbt(ph, "invt", [128, 128], F32)
            beta = sbt(ph, "beta", [128, 1], F32)
            Oc = Obj("consts0")
            for dst_t, src in ((g0r, g0_d.partition_broadcast(128)), (convw, convw_d), (pscale, pscale_d),
                               (ident_f, ident_d), (invt, inv_d), (beta, beta_d)):
                S.emit("sp", lambda e, dst_t=dst_t, src=src: e.dma_start(out=dst_t[:], in_=src), writes=[Oc], dsem="cst")
            S.emit("sp", lambda e: e.dma_start(out=poolw_f[:].bitcast(F32), in_=poolw_d), writes=[Oring[NR0 - 1]], dsem="wr%d" % (NR0 - 1))
            S.emit("dve", lambda e: e.tensor_copy(out=poolw[:], in_=poolw_f[:].bitcast(F32)), reads=[Oring[NR0 - 1], Oc], writes=[Oc])
            S.emit("dve", lambda e: e.tensor_copy(out=ident[:], in_=ident_f[:]), reads=[Oc], writes=[Oc])

            xn = [sbt(ph, "xn0", [128, 4, D], F32)] * 2
            Oxn = [Obj("xn0")] * 2
            junk = sbt(ph, "junk", [128, D], BF16)
            Ojunk = Obj("junk")
            ss = [sbt(ph, "ss%d" % i, [128, 4], F32) for i in range(2)]
            rs = [sbt(ph, "rs%d" % i, [128, 4], F32) for i in range(2)]
            rt_ = [sbt(ph, "rtmp%d" % i, [128, 4], F32) for i in range(2)]
            Oss = [Obj("ss%d" % i) for i in range(2)]
            Ors = [Obj("rs%d" % i) for i in range(2)]
            Ort = [Obj("rt%d" % i) for i in range(2)]
            hb = [sbt(ph, "hb%d" % i, [128, D], BF16) for i in range(4)]
            Ohb = [Obj("hb%d" % i) for i in range(4)]
            hT = [sbt(ph, "hT0", [128, 8, 512], BF16)] * 2
            OhT = [Obj("hT0")] * 2
            Ue = [sbt(ph, "Ue%d" % i, [128, 8, 514], BF16) for i in range(2)]
            OUe = [Obj("Ue%d" % i) for i in range(2)]
            Ve = [sbt(ph, "Ve%d" % i, [128, 8, 528], BF16) for i in range(2)]
            OVe = [Obj("Ve%d" % i) for i in range(2)]
            GAl = [sbt(ph, "GA%d" % i, [128, 8, 512], BF16) for i in range(2)]
            OGAl = [Obj("GA%d" % i) for i in range(2)]
            SZl = [sbt(ph, "SZ%d" % i, [128, 8, 512], BF16) for i in range(2)]
            OSZl = [Obj("SZ%d" % i) for i in range(2)]
            DT = sbt(ph, "DT", [128, 8, 512], BF16)
            ODT = [Obj("DT%d" % i) for i in range(8)]
            YT = sbt(ph, "YT", [128, 16, 512], BF16)
            OYT = [Obj("YT%d" % i) for i in range(16)]
            T1 = [sbt(ph, "T1_%d" % i, [128, 512], F32) for i in range(2)]
            T2 = [sbt(ph, "T2_%d" % i, [128, 512], F32) for i in range(2)]
            OT1 = [Obj("T1_%d" % i) for i in range(2)]
            OT2 = [Obj("T2_%d" % i) for i in range(2)]
            PA = [sbt(ph, "PA0", [128, 528], F32)] * 2
            PB = [sbt(ph, "PB0", [128, 528], F32)] * 2
            OPA = [Obj("PA0")] * 2
            OPB = [Obj("PB0")] * 2
            e8 = sbt(ph, "e8", [128, 8], F32)
            Oe8 = Obj("e8")
            xr = [sbt(ph, "xr%d" % i, [128, D], F32) for i in range(3)]
            Oxr = [Obj("xr%d" % i) for i in range(3)]

            def An(i, src_d, gr, Ogr):
                p = i % 2
                S.emit("sp", lambda e: e.dma_start(
                    out=xn[p][:], in_=src_d[i * 512:(i + 1) * 512, :].rearrange("(r p) d -> p r d", p=128)),
                    writes=[Oxn[p]], dsem="xn%d" % p)
                S.emit("dve", lambda e: e.memset(ss[p][:], 0.0), writes=[Oss[p]])
                for r in range(4):
                    S.emit("act", lambda e, r=r: e.activation(out=junk[:], in_=xn[p][:, r, :], func=AF.Square,
                                                              accum_out=ss[p][:, r:r + 1]),
                           reads=[Oxn[p]], writes=[Ojunk, Oss[p]])
                S.emit("dve", lambda e: e.tensor_scalar(out=ss[p][:], in0=ss[p][:], scalar1=1.0 / D, scalar2=EPS,
                                                        op0=ALU.mult, op1=ALU.add), reads=[Oss[p]], writes=[Oss[p]])
                rsqrt_dve(S, rs[p][:], ss[p][:], rt_[p][:], Ors[p], Oss[p], Ort[p])
                for r in range(4):
                    S.emit("dve", lambda e, r=r: e.scalar_tensor_tensor(
                        out=hb[r][:], in0=xn[p][:, r, :], scalar=rs[p][:, r:r + 1], in1=gr[:], op0=ALU.mult, op1=ALU.mult),
                        reads=[Oxn[p], Ors[p], Ogr], writes=[Ohb[r]])

            def At(i):
                p = i % 2
                for r in range(4):
                    bk, ob = nb()
                    bv = bk[:].bitcast(BF16)
                    for k in range(8):
                        S.emit("pe", lambda e, r=r, k=k, bv=bv: e.transpose(
                            out=bv[:, k * 128:(k + 1) * 128], in_=hb[r][:, k * 128:(k + 1) * 128], identity=ident[:]),
                            reads=[Ohb[r], Oc], writes=[ob], inc=(k == 7))
                    S.emit("act", lambda e, r=r, bv=bv: e.activation(
                        out=hT[p][:, :, r * 128:(r + 1) * 128], in_=bv.rearrange("p (k t) -> p k t", k=8), func=AF.Copy),
                        reads=[ob], writes=[OhT[p]])

            def inproj_fm(i, wsrc, chunk, evac):
                p = i % 2
                slot, oslot = wload(lambda t: (t[:].rearrange("p (k e) -> p k e", k=8),
                                               wsrc[:, chunk * 512:(chunk + 1) * 512].rearrange("(k p) e -> p k e", p=128)))
                sv = slot[:].rearrange("p (k e) -> p k e", k=8)
                for ec in range(4):
                    bk, ob = nb()
                    for k in range(8):
                        S.emit("pe", lambda e, k=k, ec=ec, bk=bk: e.matmul(
                            out=bk[:], lhsT=sv[:, k, ec * 128:(ec + 1) * 128], rhs=hT[p][:, k, :], start=(k == 0), stop=(k == 7)),
                            reads=[oslot, OhT[p]], writes=[ob], inc=(k == 7))
                    evac(bk, ob, chunk * 4 + ec)
                if wsrc is wb_in0:
                    do_casts(1)

            def B1(i):
                p = i % 2

                def ev_ah(bk, ob, ech):
                    ck = ech
                    S.emit("act", lambda e: e.activation(out=Ue[p][:, ck, 1:513], in_=bk[:], func=AF.Copy),
                           reads=[ob], writes=[OUe[p]])

                def ev_ac(bk, ob, ech):
                    ck = ech - 16
                    S.emit("dve", lambda e: e.tensor_tensor(out=Ue[p][:, ck, 1:513], in0=bk[:], in1=Ue[p][:, ck, 1:513], op=ALU.mult),
                           reads=[ob, OUe[p]], writes=[OUe[p]])

                def ev_bv(bk, ob, ech):
                    ck = ech - 32
                    S.emit("act", lambda e: e.activation(out=Ve[p][:, ck, 8:520], in_=bk[:], func=AF.Copy),
                           reads=[ob], writes=[OVe[p]])
                for half in range(2):
                    inproj_fm(i, wb_in0, 0 + half, ev_ah)
                    inproj_fm(i, wb_in0, 4 + half, ev_ac)
                for half in range(2):
                    inproj_fm(i, wb_in0, 8 + half, ev_bv)
                q = 1 - p
                t0 = i * 512
                if i == 0:
                    S.emit("dve", lambda e: e.memset(Ue[p][:, :, 0:1], 0.0), writes=[OUe[p]])
                    S.emit("dve", lambda e: e.memset(Ve[p][:, :, 0:8], 0.0), writes=[OVe[p]])
                else:
                    if t0 % LB == 0:
                        S.emit("dve", lambda e: e.tensor_scalar(out=Ue[p][:, :, 0:1], in0=Ue[q][:, :, 512:513], scalar1=beta[:, 0:1],
                                                                scalar2=None, op0=ALU.mult), reads=[OUe[q], Oc], writes=[OUe[p]])
                        S.emit("dve", lambda e: e.tensor_scalar(out=Ue[q][:, :, 513:514], in0=Ue[p][:, :, 1:2], scalar1=beta[:, 0:1],
                                                                scalar2=None, op0=ALU.mult), reads=[OUe[p], Oc], writes=[OUe[q]])
                        S.emit("dve", lambda e: e.tensor_scalar(out=Ve[p][:, :, 0:8], in0=Ve[q][:, :, 512:520], scalar1=beta[:, 0:1],
                                                                scalar2=None, op0=ALU.mult), reads=[OVe[q], Oc], writes=[OVe[p]])
                        S.emit("dve", lambda e: e.tensor_scalar(out=Ve[q][:, :, 520:528], in0=Ve[p][:, :, 8:16], scalar1=beta[:, 0:1],
                                                                scalar2=None, op0=ALU.mult), reads=[OVe[p], Oc], writes=[OVe[q]])
                    else:
                        S.emit("dve", lambda e: e.tensor_copy(out=Ue[p][:, :, 0:1], in_=Ue[q][:, :, 512:513]), reads=[OUe[q]], writes=[OUe[p]])
                        S.emit("dve", lambda e: e.tensor_copy(out=Ue[q][:, :, 513:514], in_=Ue[p][:, :, 1:2]), reads=[OUe[p]], writes=[OUe[q]])
                        S.emit("dve", lambda e: e.tensor_copy(out=Ve[p][:, :, 0:8], in_=Ve[q][:, :, 512:520]), reads=[OVe[q]], writes=[OVe[p]])
                        S.emit("dve", lambda e: e.tensor_copy(out=Ve[q][:, :, 520:528], in_=Ve[p][:, :, 8:16]), reads=[OVe[p]], writes=[OVe[q]])
                if i == NU - 1:
                    S.emit("dve", lambda e: e.memset(Ue[p][:, :, 513:514], 0.0), writes=[OUe[p]])
                    S.emit("dve", lambda e: e.memset(Ve[p][:, :, 520:528], 0.0), writes=[OVe[p]])

            def B2(i):
                GA, OGA, SZ, OSZ = GAl[i % 2], OGAl[i % 2], SZl[i % 2], OSZl[i % 2]

                def ev_az(bk, ob, ech):
                    ck = ech - 24
                    S.emit("act", lambda e: e.activation(out=GA[:, ck, :], in_=bk[:], func=AF.Silu), reads=[ob], writes=[OGA])

                def ev_ab(bk, ob, ech):
                    ck = ech - 8
                    S.emit("dve", lambda e: e.tensor_tensor(out=GA[:, ck, :], in0=bk[:], in1=GA[:, ck, :], op=ALU.mult),
                           reads=[ob, OGA], writes=[OGA])

                def ev_bz(bk, ob, ech):
                    ck = ech - 40
                    S.emit("act", lambda e: e.activation(out=SZ[:, ck, :], in_=bk[:], func=AF.Silu), reads=[ob], writes=[OSZ])
                for half in range(2):
                    inproj_fm(i, wb_in0, 10 + half, ev_bz)
                for half in range(2):
                    inproj_fm(i, wb_in0, 6 + half, ev_az)
                for half in range(2):
                    inproj_fm(i, wb_in0, 2 + half, ev_ab)

            def S2E(i):
                p = i % 2
                t0 = i * 512
                GA, OGA = GAl[i % 2], OGAl[i % 2]
                for ck in range(8):
                    b = ck % 2
                    S.emit("act", lambda e, ck=ck, b=b: e.activation(out=T1[b][:], in_=Ue[p][:, ck, 0:512], func=AF.Copy,
                                                                     scale=convw[:, ck * 3:ck * 3 + 1]),
                           reads=[OUe[p], Oc], writes=[OT1[b]])
                    S.emit("dve", lambda e, ck=ck, b=b: e.scalar_tensor_tensor(
                        out=T2[b][:], in0=Ue[p][:, ck, 1:513], scalar=convw[:, ck * 3 + 1:ck * 3 + 2], in1=T1[b][:],
                        op0=ALU.mult, op1=ALU.add), reads=[OUe[p], OT1[b], Oc], writes=[OT2[b]])
                    S.emit("dve", lambda e, ck=ck, b=b: e.scalar_tensor_tensor(
                        out=T1[b][:], in0=Ue[p][:, ck, 2:514], scalar=convw[:, ck * 3 + 2:ck * 3 + 3], in1=T2[b][:],
                        op0=ALU.mult, op1=ALU.add), reads=[OUe[p], OT2[b], Oc], writes=[OT1[b]])
                    S.emit("dve", lambda e, ck=ck, b=b: e.tensor_tensor(out=YT[:, ck, :], in0=T1[b][:], in1=GA[:, ck, :], op=ALU.mult),
                           reads=[OT1[b], OGA], writes=[OYT[ck]])
                left_kind = None
                right_kind = None
                if t0 % LB == 0:
                    left_kind = 0 if t0 == 0 else 2
                if (t0 + 512) % LB == 0:
                    right_kind = 1 if (t0 + 512) == NT else 3
                for ck in range(8):
                    g = ck // 2
                    w = 2 << g
                    b = ck % 2
                    v = Ve[p][:, ck, :]
                    S.emit("dve", lambda e, v=v, b=b: e.tensor_tensor(out=PA[b][:, 0:527], in0=v[:, 0:527], in1=v[:, 1:528], op=ALU.add),
                           reads=[OVe[p]], writes=[OPA[b]])
                    cur, Ocur, oth, Ooth = PA[b], OPA[b], PB[b], OPB[b]
                    width = 527
                    for kk in range(1, g + 1):
                        sh = 1 << kk
                        nw = width - sh
                        S.emit("dve", lambda e, cur=cur, oth=oth, nw=nw, sh=sh: e.tensor_tensor(
                            out=oth[:, 0:nw], in0=cur[:, 0:nw], in1=cur[:, sh:sh + nw], op=ALU.add),
                            reads=[Ocur], writes=[Ooth])
                        cur, Ocur, oth, Ooth = oth, Ooth, cur, Ocur
                        width = nw
                    off = 8 - w // 2
                    S.emit("dve", lambda e, cur=cur, off=off, w=w, ck=ck: e.scalar_tensor_tensor(
                        out=DT[:, ck, :], in0=cur[:, off:off + 512], scalar=1.0 / w, in1=Ve[p][:, ck, 8:520],
                        op0=ALU.mult, op1=ALU.subtract), reads=[Ocur, OVe[p]], writes=[ODT[ck]])
                    if left_kind is not None:
                        c0 = (left_kind * 4 + g) * 8
                        S.emit("dve", lambda e, cur=cur, off=off, c0=c0: e.tensor_tensor(
                            out=e8[:], in0=cur[:, off:off + 8], in1=invt[:, c0:c0 + 8], op=ALU.mult),
                            reads=[Ocur, Oc], writes=[Oe8])
                        S.emit("dve", lambda e, ck=ck: e.tensor_tensor(out=DT[:, ck, 0:8], in0=e8[:], in1=Ve[p][:, ck, 8:16], op=ALU.subtract),
                               reads=[Oe8, OVe[p]], writes=[ODT[ck]])
                    if right_kind is not None:
                        c0 = (right_kind * 4 + g) * 8
                        S.emit("dve", lambda e, cur=cur, off=off, c0=c0: e.tensor_tensor(
                            out=e8[:], in0=cur[:, off + 504:off + 512], in1=invt[:, c0:c0 + 8], op=ALU.mult),
                            reads=[Ocur, Oc], writes=[Oe8])
                        S.emit("dve", lambda e, ck=ck: e.tensor_tensor(out=DT[:, ck, 504:512], in0=e8[:], in1=Ve[p][:, ck, 512:520], op=ALU.subtract),
                               reads=[Oe8, OVe[p]], writes=[ODT[ck]])

            def S2M(i):
                p = i % 2
                t0 = i * 512
                do_casts(max(0, per_unit - 12))
                SZ, OSZ = SZl[i % 2], OSZl[i % 2]
                pwv = poolw[:].rearrange("p (g k j) -> p g k j", g=4, k=2)
                for g in range(4):
                    for jc in range(2):
                        bk, ob = nb()
                        for k2 in range(2):
                            S.emit("pe", lambda e, g=g, jc=jc, k2=k2, bk=bk: e.matmul(
                                out=bk[:], lhsT=pwv[:, g, k2, jc * 128:(jc + 1) * 128], rhs=DT[:, 2 * g + k2, :],
                                start=(k2 == 0), stop=(k2 == 1)),
                                reads=[Oc, ODT[2 * g + k2]], writes=[ob], inc=(k2 == 1))
                        ck = 2 * g + jc
                        S.emit("dve", lambda e, ck=ck, bk=bk: e.scalar_tensor_tensor(
                            out=YT[:, 8 + ck, :], in0=bk[:], scalar=pscale[:, ck:ck + 1], in1=SZ[:, ck, :],
                            op0=ALU.mult, op1=ALU.mult), reads=[ob, OSZ, Oc], writes=[OYT[8 + ck]])
                slots = []
                for qk in range(4):
                    slots.append(wload(lambda t, qk=qk: (t[:].rearrange("p (k d) -> p k d", k=4),
                                                         wb_out0[qk * 512:(qk + 1) * 512, :].rearrange("(k p) d -> p k d", p=128))))
                for r in range(4):
                    xb = (i * 4 + r) % 3
                    S.emit("sp", lambda e, r=r, xb=xb: e.dma_start(out=xr[xb][:], in_=x_d[t0 + r * 128:t0 + (r + 1) * 128, :]),
                           writes=[Oxr[xb]], dsem="xr%d" % xb)
                    for hf in range(2):
                        bk, ob = nb()
                        for ek in range(16):
                            slot, oslot = slots[ek // 4]
                            sv = slot[:].rearrange("p (k d) -> p k d", k=4)
                            S.emit("pe", lambda e, r=r, hf=hf, ek=ek, sv=sv, bk=bk: e.matmul(
                                out=bk[:], lhsT=YT[:, ek, r * 128:(r + 1) * 128], rhs=sv[:, ek % 4, hf * 512:(hf + 1) * 512],
                                start=(ek == 0), stop=(ek == 15)),
                                reads=[oslot, OYT[ek]], writes=[ob], inc=(ek == 15))
                        S.emit("dve", lambda e, hf=hf, xb=xb, bk=bk: e.tensor_tensor(
                            out=xr[xb][:, hf * 512:(hf + 1) * 512], in0=bk[:], in1=xr[xb][:, hf * 512:(hf + 1) * 512], op=ALU.add),
                            reads=[ob, Oxr[xb]], writes=[Oxr[xb]])
                    S.emit("pool", lambda e, r=r, xb=xb: e.dma_start(out=X1[t0 + r * 128:t0 + (r + 1) * 128, :], in_=xr[xb][:]),
                           reads=[Oxr[xb]], dsem="x1st%d" % xb)

            wc_f = [sbt(ph, "wc_f%d" % i, [128, 256], F32) for i in range(2)]
            wc_b = [sbt(ph, "wc_b%d" % i, [128, 256], BF16) for i in range(2)]
            Owc_f = [Obj("wc_f%d" % i) for i in range(2)]
            Owc_b = [Obj("wc_b%d" % i) for i in range(2)]
            cast_jobs = []
            for src, dst, R, C in ((w_in1, wb_in1, D, 5120), (w_out1, wb_out1, 2048, D)):
                for rb in range(R // 128):
                    for c0 in range(0, C, 256):
                        cast_jobs.append((src, dst, rb, c0))
            cj = [0]

            def cast_load(j):
                if j >= len(cast_jobs):
                    return
                src, dst, rb, c0 = cast_jobs[j]
                b = j % 2
                S.emit("pool", lambda e: e.dma_start(out=wc_f[b][:], in_=src[rb * 128:(rb + 1) * 128, c0:c0 + 256]),
                       writes=[Owc_f[b]], dsem="wcl%d" % b)

            def do_casts(n):
                for _ in range(n):
                    j = cj[0]
                    if j >= len(cast_jobs):
                        return
                    if j == 0:
                        cast_load(0)
                    cast_load(j + 1)
                    src, dst, rb, c0 = cast_jobs[j]
                    b = j % 2
                    cj[0] += 1
                    S.emit("pool", lambda e, b=b: e.tensor_copy(out=wc_b[b][:], in_=wc_f[b][:]),
                           reads=[Owc_f[b]], writes=[Owc_b[b]])
                    S.emit("pool", lambda e, b=b, dst=dst, rb=rb, c0=c0: e.dma_start(
                        out=dst[rb * 128:(rb + 1) * 128, c0:c0 + 256], in_=wc_b[b][:]), reads=[Owc_b[b]], dsem="wcs%d" % b)
            per_unit = (len(cast_jobs) + NU - 1) // NU

            An(0, x_d, g0r, Oc)
            At(0)
            for i in range(NU):
                if i + 1 < NU:
                    An(i + 1, x_d, g0r, Oc)
                B1(i)
                if i >= 1:
                    S2E(i - 1)
                B2(i)
                if i + 1 < NU:
                    At(i + 1)
                if i >= 1:
                    S2M(i - 1)
            S2E(NU - 1)
            S2M(NU - 1)
            end_phase()
        if stop_after == "L0":
            return nc

        with ExitStack() as ph:
            ring = [sbt(ph, "ringb%d" % i, [128, 4096], BF16) for i in range(NRING)]
            Oring = [Obj("ringb%d" % i) for i in range(NRING)]
            rstate = [0]

            def wload(src_ap_fn):
                i = rstate[0]
                rstate[0] = (i + 1) % NRING
                o, i_ap = src_ap_fn(ring[i])
                S.emit("sp", lambda e, o=o, i_ap=i_ap: e.dma_start(out=o, in_=i_ap), writes=[Oring[i]], dsem="wr%d" % i)
                return ring[i], Oring[i]

            g1r = sbt(ph, "g1r", [128, D], F32)
            ident_f = sbt(ph, "ident_fb", [128, 128], F32)
            ident = sbt(ph, "identb", [128, 128], BF16)
            wsT_f = sbt(ph, "wsT_f", [128, 512], F32)
            wsT = sbt(ph, "wsT", [128, 512], BF16)
            sgug = sbt(ph, "sgug", [128, 8], F32)
            bsr = sbt(ph, "bsr", [128, 2048], F32)
            Wc = sbt(ph, "Wc", [128, 8, D], BF16)
            Oc = Obj("consts1")
            for dst_ap, src in ((g1r[:], g1_d.partition_broadcast(128)), (ident_f[:], ident_d), (wsT_f[:], wsT_d),
                                (sgug[:], sgug_d), (bsr[:], bs_d.partition_broadcast(128)),
                                (Wc[:], wb_out1[0:1024, :].rearrange("(k p) d -> p k d", p=128))):
                S.emit("sp", lambda e, dst_ap=dst_ap, src=src: e.dma_start(out=dst_ap, in_=src), writes=[Oc], dsem="cst")
            S.emit("dve", lambda e: e.tensor_copy(out=wsT[:], in_=wsT_f[:]), reads=[Oc], writes=[Oc])
            S.emit("dve", lambda e: e.tensor_copy(out=ident[:], in_=ident_f[:]), reads=[Oc], writes=[Oc])

            xn = [sbt(ph, "xnb%d" % i, [128, 4, D], F32) for i in range(2)]
            Oxn = [Obj("xnb%d" % i) for i in range(2)]
            junk = sbt(ph, "junkb", [128, D], BF16)
            Ojunk = Obj("junkb")
            ss = [sbt(ph, "ssb%d" % i, [128, 4], F32) for i in range(2)]
            rs = [sbt(ph, "rsb%d" % i, [128, 4], F32) for i in range(2)]
            rt_ = [sbt(ph, "rtmpb%d" % i, [128, 4], F32) for i in range(2)]
            Oss = [Obj("ssb%d" % i) for i in range(2)]
            Ors = [Obj("rsb%d" % i) for i in range(2)]
            Ort = [Obj("rtb%d" % i) for i in range(2)]
            hb = [sbt(ph, "hbb%d" % i, [128, D], BF16) for i in range(4)]
            Ohb = [Obj("hbb%d" % i) for i in range(4)]
            hT = [sbt(ph, "hTb%d" % i, [128, 8, 512], BF16) for i in range(2)]
            OhT = [Obj("hTb%d" % i) for i in range(2)]
            GC = sbt(ph, "GC", [128, 8, 512], BF16)
            OGC = Obj("GC")
            VR = sbt(ph, "VR", [128, 4, D], F32)
            OVR = [Obj("VR%d" % i) for i in range(4)]
            VN = sbt(ph, "VN", [128, 4, D], BF16)
            OVN = [Obj("VN%d" % i) for i in range(4)]
            st6 = sbt(ph, "st6", [128, 16, 6], F32)
            Ost6 = Obj("st6")
            mv = sbt(ph, "mv", [128, 16, 2], F32)
            Omv = Obj("mv")
            sums = sbt(ph, "sums", [128, 16], F32)
            sumsq = sbt(ph, "sumsq", [128, 16], F32)
            mean16 = sbt(ph, "mean16", [128, 16], F32)
            msq16 = sbt(ph, "msq16", [128, 16], F32)
            Osums, Osumsq, Omean16, Omsq16 = Obj("sums"), Obj("sumsq"), Obj("mean16"), Obj("msq16")
            var_e = sbt(ph, "var_e", [128, 16], F32)
            rstd16 = sbt(ph, "rstd16", [128, 16], F32)
            rtmp16 = sbt(ph, "rtmp16", [128, 16], F32)
            Ovar = Obj("var_e")
            Orstd16 = Obj("rstd16")
            Ortmp16 = Obj("rtmp16")
            FS = sbt(ph, "FS", [128, 4, D], BF16)
            OFS = Obj("FS")
            GS = sbt(ph, "GS", [128, 4, D], BF16)
            OGS = Obj("GS")
            TM = [sbt(ph, "TM%d" % i, [128, 512], F32) for i in range(2)]
            OTM = [Obj("TM%d" % i) for i in range(2)]
            YC = sbt(ph, "YC", [128, 8, 512], BF16)
            OYC = [Obj("YC%d" % i) for i in range(8)]
            x1ps = [sbt(ph, "x1ps%d" % i, [128, D], F32) for i in range(2)]
            Ox1ps = [Obj("x1ps%d" % i) for i in range(2)]

            def An1(i):
                p = i % 2
                S.emit("sp", lambda e: e.dma_start(
                    out=xn[p][:], in_=X1[i * 512:(i + 1) * 512, :].rearrange("(r p) d -> p r d", p=128)),
                    writes=[Oxn[p]], dsem="xn%d" % p)
                S.emit("dve", lambda e: e.memset(ss[p][:], 0.0), writes=[Oss[p]])
                for r in range(4):
                    S.emit("act", lambda e, r=r: e.activation(out=junk[:], in_=xn[p][:, r, :], func=AF.Square,
                                                              accum_out=ss[p][:, r:r + 1]),
                           reads=[Oxn[p]], writes=[Ojunk, Oss[p]])
                S.emit("dve", lambda e: e.tensor_scalar(out=ss[p][:], in0=ss[p][:], scalar1=1.0 / D, scalar2=EPS,
                                                        op0=ALU.mult, op1=ALU.add), reads=[Oss[p]], writes=[Oss[p]])
                rsqrt_dve(S, rs[p][:], ss[p][:], rt_[p][:], Ors[p], Oss[p], Ort[p])
                for r in range(4):
                    S.emit("dve", lambda e, r=r: e.scalar_tensor_tensor(
                        out=hb[r][:], in0=xn[p][:, r, :], scalar=rs[p][:, r:r + 1], in1=g1r[:], op0=ALU.mult, op1=ALU.mult),
                        reads=[Oxn[p], Ors[p], Oc], writes=[Ohb[r]])

            def At1(i):
                p = i % 2
                for r in range(4):
                    bk, ob = nb()
                    bv = bk[:].bitcast(BF16)
                    for k in range(8):
                        S.emit("pe", lambda e, r=r, k=k, bv=bv: e.transpose(
                            out=bv[:, k * 128:(k + 1) * 128], in_=hb[r][:, k * 128:(k + 1) * 128], identity=ident[:]),
                            reads=[Ohb[r], Oc], writes=[ob], inc=(k == 7))
                    S.emit("act", lambda e, r=r, bv=bv: e.activation(
                        out=hT[p][:, :, r * 128:(r + 1) * 128], in_=bv.rearrange("p (k t) -> p k t", k=8), func=AF.Copy),
                        reads=[ob], writes=[OhT[p]])

            def inproj_fm1(i, chunk, evac):
                p = i % 2
                slot, oslot = wload(lambda t: (t[:].rearrange("p (k e) -> p k e", k=8),
                                               wb_in1[:, chunk * 512:(chunk + 1) * 512].rearrange("(k p) e -> p k e", p=128)))
                sv = slot[:].rearrange("p (k e) -> p k e", k=8)
                for ec in range(4):
                    bk, ob = nb()
                    for k in range(8):
                        S.emit("pe", lambda e, k=k, ec=ec, bk=bk: e.matmul(
                            out=bk[:], lhsT=sv[:, k, ec * 128:(ec + 1) * 128], rhs=hT[p][:, k, :], start=(k == 0), stop=(k == 7)),
                            reads=[oslot, OhT[p]], writes=[ob], inc=(k == 7))
                    evac(bk, ob, chunk * 4 + ec)

            def inproj_tm1(i, chunk, evac, after=None):
                p = i % 2
                slot, oslot = wload(lambda t: (t[:].rearrange("p (k e) -> p k e", k=8),
                                               wb_in1[:, chunk * 512:(chunk + 1) * 512].rearrange("(k p) e -> p k e", p=128)))
                sv = slot[:].rearrange("p (k e) -> p k e", k=8)
                for r in range(4):
                    bk, ob = nb()
                    for k in range(8):
                        S.emit("pe", lambda e, k=k, r=r, bk=bk: e.matmul(
                            out=bk[:], lhsT=hT[p][:, k, r * 128:(r + 1) * 128], rhs=sv[:, k, :], start=(k == 0), stop=(k == 7)),
                            reads=[oslot, OhT[p]], writes=[ob], inc=(k == 7))
                    evac(bk, ob, r)
                    if after is not None:
                        after(r)

            def L1A(i):
                t0 = i * 512
                p = i % 2

                def ev_cz(bk, ob, ech):
                    ck = ech - 16
                    S.emit("act", lambda e: e.activation(out=GC[:, ck, :], in_=bk[:], func=AF.Silu), reads=[ob], writes=[OGC])

                def ev_cu(bk, ob, ech):
                    ck = ech
                    S.emit("dve", lambda e: e.tensor_tensor(out=GC[:, ck, :], in0=bk[:], in1=GC[:, ck, :], op=ALU.mult),
                           reads=[ob, OGC], writes=[OGC])
                for half in range(2):
                    inproj_fm1(i, 4 + half, ev_cz)
                    inproj_fm1(i, 0 + half, ev_cu)
                if i + 1 < NU:
                    An1(i + 1)
                if lim < 0.3:
                    return
                S.emit("dve", lambda e: e.memset(sums[:], 0.0), writes=[Osums])
                S.emit("dve", lambda e: e.memset(sumsq[:], 0.0), writes=[Osumsq])
                for half in range(2):
                    def ev_cv(bk, ob, r, half=half):
                        for hh in range(2):
                            idx = r * 4 + half * 2 + hh
                            c0 = (half * 2 + hh) * 256
                            S.emit("act", lambda e, hh=hh, idx=idx, c0=c0: e.activation(
                                out=VR[:, r, c0:c0 + 256], in_=bk[:, hh * 256:(hh + 1) * 256], func=AF.Copy,
                                accum_out=sums[:, idx:idx + 1]), reads=[ob], writes=[OVR[r], Osums])
                            S.emit("act", lambda e, hh=hh, idx=idx: e.activation(
                                out=junk[:, 0:256], in_=bk[:, hh * 256:(hh + 1) * 256], func=AF.Square,
                                accum_out=sumsq[:, idx:idx + 1]), reads=[ob], writes=[Ojunk, Osumsq])
                    inproj_tm1(i, 2 + half, ev_cv)
                if lim < 0.7:
                    return
                S.emit("dve", lambda e: e.tensor_scalar(out=mean16[:], in0=sums[:], scalar1=1.0 / 256, scalar2=None, op0=ALU.mult),
                       reads=[Osums], writes=[Omean16])
                S.emit("dve", lambda e: e.tensor_tensor(out=msq16[:], in0=mean16[:], in1=mean16[:], op=ALU.mult),
                       reads=[Omean16], writes=[Omsq16])
                S.emit("dve", lambda e: e.scalar_tensor_tensor(out=var_e[:], in0=sumsq[:], scalar=1.0 / 256, in1=msq16[:],
                                                               op0=ALU.mult, op1=ALU.subtract), reads=[Osumsq, Omsq16], writes=[Ovar])
                S.emit("dve", lambda e: e.tensor_scalar(out=var_e[:], in0=var_e[:], scalar1=EPS, scalar2=None, op0=ALU.add),
                       reads=[Ovar], writes=[Ovar])
                rsqrt_dve(S, rstd16[:], var_e[:], rtmp16[:], Orstd16, Ovar, Ortmp16)
                if lim < 0.9:
                    return
                for r in range(4):
                    for h in range(4):
                        idx = r * 4 + h
                        S.emit("dve", lambda e, r=r, h=h, idx=idx: e.tensor_scalar(
                            out=VN[:, r, h * 256:(h + 1) * 256], in0=VR[:, r, h * 256:(h + 1) * 256],
                            scalar1=mean16[:, idx:idx + 1], scalar2=rstd16[:, idx:idx + 1], op0=ALU.subtract, op1=ALU.mult),
                            reads=[OVR[r], Omean16, Orstd16], writes=[OVN[r]])
                if lim < 2:
                    return
                for half in range(2):
                    def ev_df(bk, ob, r, half=half):
                        S.emit("act", lambda e: e.activation(out=FS[:, r, half * 512:(half + 1) * 512], in_=bk[:], func=AF.Copy),
                               reads=[ob], writes=[OFS])
                    inproj_tm1(i, 6 + half, ev_df)
                S.emit("pool", lambda e: e.dma_start(out=F1[t0:t0 + 512, :].rearrange("(r p) d -> p r d", p=128), in_=FS[:]),
                       reads=[OFS], dsem="f1st")
                for r in range(4):
                    t = t0 + r * 128
                    bhi = t // LB
                    rr = t % LB
                    a0 = rr // NB4
                    row0 = NB * a0 + NB4 * bhi
                    na = 128 // NB4
                    dst = bass.AP(F2.tensor, row0 * D, [[NB * D, na], [D, NB4], [1, D]])
                    S.emit("pool", lambda e, r=r, dst=dst: e.dma_start(out=dst, in_=FS[:, r, :]), reads=[OFS], dsem="f2st")
                if lim < 4:
                    return
                wsv = wsT[:].rearrange("p (h q) -> p h q", h=4)
                bsv = bsr[:].rearrange("p (h c) -> p h c", h=4)

                def sgu_ck(ck):
                    h = ck // 2
                    b = ck % 2
                    bk, ob = nb()
                    for r in range(4):
                        S.emit("pe", lambda e, r=r, bk=bk: e.matmul(
                            out=bk[:, r * 128:(r + 1) * 128], lhsT=VN[:, r, ck * 128:(ck + 1) * 128], rhs=wsv[:, h, :],
                            start=True, stop=True), reads=[OVN[r], Oc], writes=[ob], inc=(r == 3))
                    S.emit("dve", lambda e, bk=bk: e.scalar_tensor_tensor(
                        out=TM[b][:], in0=bk[:], scalar=sgug[:, ck:ck + 1], in1=bsv[:, h, :], op0=ALU.mult, op1=ALU.add),
                        reads=[ob, Oc], writes=[OTM[b]])
                    S.emit("dve", lambda e: e.tensor_tensor(out=YC[:, ck, :], in0=TM[b][:], in1=GC[:, ck, :], op=ALU.mult),
                           reads=[OTM[b], OGC], writes=[OYC[ck]])
                for half in range(2):
                    def ev_dz(bk, ob, r, half=half):
                        S.emit("act", lambda e: e.activation(out=GS[:, r, half * 512:(half + 1) * 512], in_=bk[:], func=AF.Silu),
                               reads=[ob], writes=[OGS])
                    inproj_tm1(i, 8 + half, ev_dz, after=(None if half == 0 else (lambda r: (sgu_ck(2 * r), sgu_ck(2 * r + 1)))))
                S.emit("pool", lambda e: e.dma_start(out=G[t0:t0 + 512, :].rearrange("(r p) d -> p r d", p=128), in_=GS[:]),
                       reads=[OGS], dsem="gst")
                if i + 1 < NU:
                    At1(i + 1)
                if lim < 5:
                    return
                for r in range(4):
                    xb = (i * 4 + r) % 2
                    for hf in range(2):
                        bk, ob = nb()
                        for ek in range(8):
                            S.emit("pe", lambda e, r=r, hf=hf, ek=ek, bk=bk: e.matmul(
                                out=bk[:], lhsT=YC[:, ek, r * 128:(r + 1) * 128], rhs=Wc[:, ek, hf * 512:(hf + 1) * 512],
                                start=(ek == 0), stop=(ek == 7)), reads=[Oc, OYC[ek]], writes=[ob], inc=(ek == 7))
                        S.emit("dve", lambda e, r=r, hf=hf, xb=xb, bk=bk: e.tensor_tensor(
                            out=x1ps[xb][:, hf * 512:(hf + 1) * 512], in0=bk[:], in1=xn[p][:, r, hf * 512:(hf + 1) * 512], op=ALU.add),
                            reads=[ob, Oxn[p]], writes=[Ox1ps[xb]])
                    S.emit("pool", lambda e, r=r, xb=xb: e.dma_start(out=X1P[t0 + r * 128:t0 + (r + 1) * 128, :], in_=x1ps[xb][:]),
                           reads=[Ox1ps[xb]], dsem="x1pst%d" % xb)

            An1(0)
            At1(0)
            for i in range(NU):
                L1A(i)
            end_phase()
        if stop_after == "L1A":
            return nc

        gfr = sbt(st, "gfr", [128, D], F32)
        ident_f = sbt(st, "ident_fc", [128, 128], F32)
        ident = sbt(st, "identc", [128, 128], BF16)
        DCt = sbt(st, "DCt", [128, 1024], F32)
        fw = sbt(st, "fw", [128, 2048], F32)
        M2 = sbt(st, "M2", [128, 4, 4, 256], BF16)
        Wd = sbt(st, "Wd", [128, 8, D], BF16)
        OcC = Obj("constsC")
        for dst_ap, src in ((gfr[:], gf_d.partition_broadcast(128)), (ident_f[:], ident_d), (DCt[:], DC_d), (fw[:], fnetw_d),
                            (Wd[:], wb_out1[1024:2048, :].rearrange("(k p) d -> p k d", p=128))):
            S.emit("sp", lambda e, dst_ap=dst_ap, src=src: e.dma_start(out=dst_ap, in_=src), writes=[OcC], dsem="cst")
        S.emit("dve", lambda e: e.tensor_copy(out=ident[:], in_=ident_f[:]), reads=[OcC], writes=[OcC])
        DCv = DCt[:].rearrange("p (k m c) -> p k m c", k=2, m=2)
        fwv = fw[:].rearrange("p (g k j) -> p g k j", g=4, k=2)
        for g in range(4):
            for pq in range(2):
                for c2 in range(2):
                    bk, ob = nb()
                    for k in range(2):
                        S.emit("pe", lambda e, g=g, pq=pq, c2=c2, k=k, bk=bk: e.matmul(
                            out=bk[:, 0:256], lhsT=DCv[:, k, pq, c2 * 128:(c2 + 1) * 128], rhs=fwv[:, g, k, :],
                            start=(k == 0), stop=(k == 1)), reads=[OcC], writes=[ob], inc=(k == 1))
                    S.emit("act", lambda e, g=g, pq=pq, c2=c2, bk=bk: e.activation(
                        out=M2[:, g, pq * 2 + c2, :], in_=bk[:, 0:256], func=AF.Copy), reads=[ob], writes=[OcC])


        with ExitStack() as ph:
            MA_f = sbt(ph, "MA_f", [128, 512], F32)
            MAb = sbt(ph, "MAb", [128, 512], BF16)
            Oc = Obj("constsA")
            S.emit("sp", lambda e: e.dma_start(out=MA_f[:], in_=MA_d), writes=[Oc], dsem="cst")
            S.emit("dve", lambda e: e.tensor_copy(out=MAb[:], in_=MA_f[:]), reads=[Oc], writes=[Oc])
            MAv = MAb[:].rearrange("p (m k) -> p m k", m=4)
            F1t = [sbt(ph, "F1t%d" % i, [128, BG, 512], BF16) for i in range(2)]
            F2t = [sbt(ph, "F2t%d" % i, [128, BG, 512], BF16) for i in range(2)]
            OF1t = [Obj("F1t%d" % i) for i in range(2)]
            OF2t = [Obj("F2t%d" % i) for i in range(2)]
            TSB = 4
            TS = [sbt(ph, "TS%d" % i, [128, 2, TSB, 512], BF16) for i in range(3)]
            OTS = [Obj("TS%d" % i) for i in range(3)]
            F1v = F1.rearrange("(a b) c -> a b c", b=NB)
            F2v = F2.rearrange("(a b) c -> a b c", b=NB)
            Tv = Tscr.rearrange("k (m b c) -> k m b c", m=2, b=NB)
            n = 0
            tsn = 0
            for cb in range(2):
                for bg in range(NBG):
                    p = n % 2
                    n += 1
                    S.emit("sp", lambda e, p=p, bg=bg, cb=cb: e.dma_start(
                        out=F1t[p][:], in_=F1v[:, bg * BG:(bg + 1) * BG, cb * 512:(cb + 1) * 512]), writes=[OF1t[p]], dsem="f1l%d" % p)
                    S.emit("sp", lambda e, p=p, bg=bg, cb=cb: e.dma_start(
                        out=F2t[p][:], in_=F2v[:, bg * BG:(bg + 1) * BG, cb * 512:(cb + 1) * 512]), writes=[OF2t[p]], dsem="f2l%d" % p)
                    for bl in range(BG):
                        b_abs = bg * BG + bl
                        tb = (tsn // TSB) % 3
                        bi = tsn % TSB
                        tsn += 1
                        for m in range(2):
                            bk, ob = nb()
                            S.emit("pe", lambda e, m=m, p=p, bl=bl, bk=bk: e.matmul(
                                out=bk[:], lhsT=MAv[:, m, :], rhs=F1t[p][:, bl, :], start=True, stop=False),
                                reads=[Oc, OF1t[p]], writes=[ob], inc=False)
                            S.emit("pe", lambda e, m=m, p=p, bl=bl, bk=bk: e.matmul(
                                out=bk[:], lhsT=MAv[:, 2 + m, :], rhs=F2t[p][:, bl, :], start=False, stop=True),
                                reads=[Oc, OF2t[p]], writes=[ob], inc=True)
                            if m == 0:
                                S.emit("act", lambda e, tb=tb, bi=bi, bk=bk: e.activation(out=TS[tb][:, 0, bi, :], in_=bk[:], func=AF.Copy),
                                       reads=[ob], writes=[OTS[tb]])
                            else:
                                S.emit("dve", lambda e, tb=tb, bi=bi, bk=bk: e.tensor_copy(out=TS[tb][:, 1, bi, :], in_=bk[:]),
                                       reads=[ob], writes=[OTS[tb]])
                        if bi == TSB - 1:
                            b0 = b_abs - (TSB - 1)
                            S.emit("pool", lambda e, tb=tb, b0=b0, cb=cb: e.dma_start(
                                out=Tv[:, :, b0:b0 + TSB, cb * 512:(cb + 1) * 512], in_=TS[tb][:]), reads=[OTS[tb]], dsem="tst%d" % tb)
            end_phase()
        if stop_after == "FA":
            return nc

        with ExitStack() as ph:
            Oc = OcC
            if KU <= 4:
                Tt = [sbt(ph, "Tt%d" % i, [KP, KU, D], BF16) for i in range(2)]
                OTt = [Obj("Tt%d" % i) for i in range(2)]
            else:
                Tt = [sbt(ph, "Tt0", [KP, KU, D], BF16)] * 2
                OTt = [Obj("Tt0")] * 2
            MCf = [sbt(ph, "MCf%d" % i, [KP, KU * 2 * NB], F32) for i in range(3)]
            MCb = [sbt(ph, "MCb%d" % i, [KP, KU * 2 * NB], BF16) for i in range(3)]
            OMCf = [Obj("MCf%d" % i) for i in range(3)]
            OMCb = [Obj("MCb%d" % i) for i in range(3)]
            PQT = sbt(ph, "PQT", [128, 8, 2, 512], BF16)
            OPQ = [Obj("PQ%d" % i) for i in range(8)]
            Gr = [sbt(ph, "Gr%d" % i, [128, D], BF16) for i in range(4)]
            OGr = [Obj("Gr%d" % i) for i in range(4)]
            GZT = sbt(ph, "GZT", [128, 8, 512], BF16)
            OGZT = Obj("GZT")
            YDTl = [sbt(ph, "YDT%d" % i, [128, 8, 512], BF16) for i in range(2)]
            OYDl = [[Obj("YD%d_%d" % (j, i)) for i in range(8)] for j in range(2)]
            x1r = [sbt(ph, "x1r%d" % i, [128, D], F32) for i in range(2)]
            Ox1r = [Obj("x1r%d" % i) for i in range(2)]
            X2 = sbt(ph, "X2", [128, 4, D], F32)
            OX2 = [Obj("X2_%d" % i) for i in range(4)]
            junk = sbt(ph, "junkc", [128, D], BF16)
            Ojunk = Obj("junkc")
            ssf = sbt(ph, "ssf", [128, 4], F32)
            rsf = sbt(ph, "rsf", [128, 4], F32)
            rtf = sbt(ph, "rtf", [128, 4], F32)
            Ossf, Orsf, Ortf = Obj("ssf"), Obj("rsf"), Obj("rtf")
            outs = [sbt(ph, "outs%d" % i, [128, D], F32) for i in range(4)]
            Oouts = [Obj("outs%d" % i) for i in range(4)]
            Tsv = Tscr.rearrange("k (q c) -> q k c", c=D)
            MCv = MC_d.rearrange("q (k n) -> q k n", n=2 * NB)

            K2H = NB // 2

            def rows_ap(tensor_ap, u, r):
                kq = r // 2
                k2_0 = (r % 2) * K2H
                base = u * KU + kq * KPB + 128 * k2_0
                return bass.AP(tensor_ap.tensor, base * D, [[128 * D, K2H], [D, KPB], [1, D]])

            def fc_L(u):
                p = u % 2
                S.emit("sp", lambda e: e.dma_start(out=Tt[p][:], in_=Tsv[:, u * KU:(u + 1) * KU, :]),
                       writes=[OTt[p]], dsem="ttl%d" % p)

            def fc_MCl(u):
                p = u % 3
                S.emit("sp", lambda e: e.dma_start(out=MCf[p][:].rearrange("q (k n) -> q k n", n=2 * NB),
                                                   in_=MCv[:, u * KU:(u + 1) * KU, :]), writes=[OMCf[p]], dsem="mcl%d" % p)

            def fc_MCc(u):
                p = u % 3
                S.emit("dve", lambda e: e.tensor_copy(out=MCb[p][:], in_=MCf[p][:]), reads=[OMCf[p]], writes=[OMCb[p]])

            def fc_Gl(u):
                for r in range(4):
                    gb = r
                    S.emit("sp", lambda e, gb=gb, r=r: e.dma_start(out=Gr[gb][:], in_=rows_ap(G, u, r)),
                           writes=[OGr[gb]], dsem="grl%d" % gb)

            def fc_Gt(u):
                for r in range(4):
                    gb = r
                    bk, ob = nb()
                    bv = bk[:].bitcast(BF16)
                    for k in range(8):
                        S.emit("pe", lambda e, gb=gb, k=k, bv=bv: e.transpose(
                            out=bv[:, k * 128:(k + 1) * 128], in_=Gr[gb][:, k * 128:(k + 1) * 128], identity=ident[:]),
                            reads=[OGr[gb], Oc], writes=[ob], inc=(k == 7))
                    S.emit("act", lambda e, r=r, bv=bv: e.activation(
                        out=GZT[:, :, r * 128:(r + 1) * 128], in_=bv.rearrange("p (k t) -> p k t", k=8), func=AF.Copy),
                        reads=[ob], writes=[OGZT])

            def fc_C(u):
                p = u % 2
                pm = u % 3
                MCbv = MCb[pm][:].rearrange("q (k n) -> q k n", n=2 * NB)
                for ck in range(8):
                    for kq in range(KU // KPB):
                        bk, ob = nb()
                        bkv = bk[:].rearrange("p (q n k) -> p k q n", q=2, n=NB, k=KPB)
                        for kk in range(KPB):
                            k1l = kq * KPB + kk
                            S.emit("pe", lambda e, ck=ck, kk=kk, k1l=k1l, bkv=bkv: e.matmul(
                                out=bkv[:, kk, :, :], lhsT=Tt[p][:, k1l, ck * 128:(ck + 1) * 128],
                                rhs=MCbv[:, k1l, :], start=True, stop=True),
                                reads=[OTt[p], OMCb[pm]], writes=[ob], inc=(kk == KPB - 1))
                        lo = kq * KPB * NB
                        hi = (kq + 1) * KPB * NB
                        outv = PQT[:, ck, :, lo:hi]
                        inv_ = bk[:].rearrange("p (q m) -> p q m", q=2)
                        S.emit("act", lambda e, outv=outv, inv_=inv_: e.activation(out=outv, in_=inv_, func=AF.Copy),
                               reads=[ob], writes=[OPQ[ck]])

            def fc_D(u):
                YDT, OYD = YDTl[u % 2], OYDl[u % 2]
                for g in range(4):
                    for jc in range(2):
                        bk, ob = nb()
                        n_ = 0
                        for pq in range(2):
                            for c2 in range(2):
                                S.emit("pe", lambda e, g=g, jc=jc, pq=pq, c2=c2, bk=bk, n_=n_: e.matmul(
                                    out=bk[:], lhsT=M2[:, g, pq * 2 + c2, jc * 128:(jc + 1) * 128], rhs=PQT[:, 2 * g + c2, pq, :],
                                    start=(n_ == 0), stop=(n_ == 3)), reads=[Oc, OPQ[2 * g + c2]], writes=[ob], inc=(n_ == 3))
                                n_ += 1
                        ck = 2 * g + jc
                        S.emit("dve", lambda e, ck=ck, bk=bk: e.tensor_tensor(out=YDT[:, ck, :], in0=bk[:], in1=GZT[:, ck, :], op=ALU.mult),
                               reads=[ob, OGZT], writes=[OYD[ck]])

            def fc_O(u, rows=(0, 1, 2, 3)):
                YDT, OYD = YDTl[u % 2], OYDl[u % 2]
                if rows[0] == 0:
                    S.emit("dve", lambda e: e.memset(ssf[:], 0.0), writes=[Ossf])
                for r in rows:
                    xb = r % 2
                    S.emit("sp", lambda e, xb=xb, r=r: e.dma_start(out=x1r[xb][:], in_=rows_ap(X1P, u, r)),
                           writes=[Ox1r[xb]], dsem="x1rl%d" % xb)
                    for hf in range(2):
                        bk, ob = nb()
                        for ek in range(8):
                            S.emit("pe", lambda e, r=r, hf=hf, ek=ek, bk=bk: e.matmul(
                                out=bk[:], lhsT=YDT[:, ek, r * 128:(r + 1) * 128], rhs=Wd[:, ek, hf * 512:(hf + 1) * 512],
                                start=(ek == 0), stop=(ek == 7)), reads=[Oc, OYD[ek]], writes=[ob], inc=(ek == 7))
                        S.emit("dve", lambda e, r=r, hf=hf, xb=xb, bk=bk: e.tensor_tensor(
                            out=X2[:, r, hf * 512:(hf + 1) * 512], in0=bk[:], in1=x1r[xb][:, hf * 512:(hf + 1) * 512], op=ALU.add),
                            reads=[ob, Ox1r[xb]], writes=[OX2[r]])
                    S.emit("act", lambda e, r=r: e.activation(out=junk[:], in_=X2[:, r, :], func=AF.Square, accum_out=ssf[:, r:r + 1]),
                           reads=[OX2[r]], writes=[Ojunk, Ossf])

            def fc_O2(u):
                S.emit("dve", lambda e: e.tensor_scalar(out=ssf[:], in0=ssf[:], scalar1=1.0 / D, scalar2=EPS, op0=ALU.mult, op1=ALU.add),
                       reads=[Ossf], writes=[Ossf])
                rsqrt_dve(S, rsf[:], ssf[:], rtf[:], Orsf, Ossf, Ortf)
                for r in range(4):
                    ob_ = r
                    S.emit("dve", lambda e, r=r, ob_=ob_: e.scalar_tensor_tensor(
                        out=outs[ob_][:], in0=X2[:, r, :], scalar=rsf[:, r:r + 1], in1=gfr[:], op0=ALU.mult, op1=ALU.mult),
                        reads=[OX2[r], Orsf, Oc], writes=[Oouts[ob_]])
                    S.emit("pool", lambda e, r=r, ob_=ob_: e.dma_start(out=rows_ap(y_d, u, r), in_=outs[ob_][:]),
                           reads=[Oouts[ob_]], dsem="yst%d" % ob_)

            fc_L(0)
            fc_MCl(0)
            fc_MCc(0)
            if NU > 1:
                fc_MCl(1)
            fc_Gl(0)
            for u in range(NU):
                fc_C(u)
                if u + 1 < NU:
                    fc_L(u + 1)
                    fc_MCc(u + 1)
                if u + 2 < NU:
                    fc_MCl(u + 2)
                fc_Gt(u)
                if u + 1 < NU:
                    fc_Gl(u + 1)
                if u >= 1:
                    fc_O(u - 1, rows=(0, 1))
                fc_D(u)
                if u >= 1:
                    fc_O(u - 1, rows=(2, 3))
                    fc_O2(u - 1)
            fc_O(NU - 1)
            fc_O2(NU - 1)
            end_phase()
    return nc


def _tables(NB, ctype):
    NT = 128 * NB
    a = np.arange(128)
    ang = 2 * np.pi * np.outer(a, a) / 128.0
    MAr = np.cos(ang)
    MAi = -np.sin(ang)
    Z = np.zeros_like(MAr)
    if ctype == 1:
        MA = np.stack([MAr, MAi, Z, Z], axis=1)
    else:
        MA = np.stack([Z, Z, MAr, MAi], axis=1)
    MA = MA.reshape(128, 512).astype(np.float32)
    k1 = np.arange(128)[None, :, None]
    if ctype == 1:
        S_ = NT
        b = np.arange(NB)[:, None, None]
        k2 = np.arange(NB)[None, None, :]
        th = 2 * np.pi * (k1 * b / S_ + k2 * b / NB)
        c = np.cos(th) / np.sqrt(S_)
        s = np.sin(th) / np.sqrt(S_)
    else:
        NB4 = NB // 4
        S_ = NT // 4
        b = np.arange(NB)[:, None, None]
        k2 = np.arange(NB)[None, None, :]
        sb_, s2 = b // NB4, b % NB4
        sk, kk2 = k2 // NB4, k2 % NB4
        th = 2 * np.pi * (k1 * s2 / S_ + kk2 * s2 / NB4)
        mask = (sb_ == sk).astype(np.float64)
        c = np.cos(th) * mask / np.sqrt(S_)
        s = np.sin(th) * mask / np.sqrt(S_)
    MC = np.zeros((2, NB, 128, 2, NB))
    MC[0, :, :, 0, :] = c
    MC[1, :, :, 0, :] = s
    MC[0, :, :, 1, :] = s
    MC[1, :, :, 1, :] = -c
    MC = MC.reshape(2 * NB, 128 * 2 * NB).astype(np.float32)
    cc = np.arange(256)
    angc = 2 * np.pi * np.outer(cc, cc) / 256.0
    DCfull = np.stack([np.cos(angc) / 16.0, -np.sin(angc) / 16.0], axis=1)
    DC = DCfull.reshape(2, 128, 2, 256).transpose(1, 0, 2, 3).reshape(128, 1024).astype(np.float32)
    inv = np.zeros((4, 4, 8))
    for g in range(4):
        w = 2 << g
        for j in range(8):
            cl = min(j + w // 2, 10 ** 9) - max(j - w // 2, 0)
            cr = min(8 - j, w // 2) + w // 2
            inv[0, g, j] = 1.0 / cl
            inv[1, g, j] = 1.0 / cr
            inv[2, g, j] = 1.0 / cl if ctype == 2 else 1.0 / w
            inv[3, g, j] = 1.0 / cr if ctype == 2 else 1.0 / w
    inv_tab = np.broadcast_to(inv.reshape(1, 128), (128, 128)).astype(np.float32).copy()
    beta = np.full((128, 1), 1.0 if ctype == 1 else 0.0, np.float32)
    ident = np.eye(128, dtype=np.float32)
    return dict(MA=MA, MC=MC, DC=DC, inv_tab=inv_tab, beta=beta, ident=ident)


def _weight_layouts(inp):
    f = np.float32
    conv_w = np.asarray(inp["ev_conv_w"][0], f)
    convw_l = conv_w.reshape(3, 8, 128).transpose(2, 1, 0).reshape(128, 24)
    pscale_l = np.asarray(inp["ev_pool_scale"][0], f).reshape(8, 128).T
    pw = np.asarray(inp["ev_pool_w"][0], f)
    poolw_l = pw.reshape(4, 2, 128, 256).transpose(2, 0, 1, 3).reshape(128, 2048)
    ws = np.asarray(inp["od_sgu_ws"][0], f)
    wsT_l = ws.transpose(2, 0, 1).reshape(128, 512)
    sgug_l = np.asarray(inp["od_sgu_norm_g"][0], f).reshape(8, 128).T
    bs = np.asarray(inp["od_sgu_bs"][0], f)
    bs_l = np.broadcast_to(bs[:, None, :], (4, 4, 128)).reshape(2048)
    fw = np.asarray(inp["od_fnet_w"][0], f)
    fnetw_l = fw.reshape(4, 2, 128, 256).transpose(2, 0, 1, 3).reshape(128, 2048)
    c = np.ascontiguousarray
    return dict(
        w_in0=c(np.asarray(inp["ev_w_in"][0], f)), w_out0=c(np.asarray(inp["ev_w_out"][0], f)),
        w_in1=c(np.asarray(inp["od_w_in"][0], f)), w_out1=c(np.asarray(inp["od_w_out"][0], f)),
        g0=c(np.asarray(inp["norm_g"][0], f)), g1=c(np.asarray(inp["norm_g"][1], f)), gf=c(np.asarray(inp["final_g"], f)),
        convw_l=c(convw_l), pscale_l=c(pscale_l), poolw_l=c(poolw_l), wsT_l=c(wsT_l), sgug_l=c(sgug_l),
        bs_l=c(bs_l), fnetw_l=c(fnetw_l))


_NC_CACHE = {}


def run(inp, debug=False, stop_after=None, lim=99):
    xp = np.asarray(inp["x_prompt"], np.float32)
    xs = np.asarray(inp["x_sample"], np.float32)
    BS, SS = xs.shape[0], xs.shape[1]
    BP, SP = xp.shape[0], xp.shape[1]
    assert BS == 4 and BP == 8 and SS == 4 * SP and SS % 512 == 0
    NT = SS
    NB = NT // 128
    key = (NB, debug, stop_after, lim)
    if key not in _NC_CACHE:
        _NC_CACHE[key] = build(NB, debug=debug, stop_after=stop_after, lim=lim)
    nc = _NC_CACHE[key]
    wl = _weight_layouts(inp)
    t1 = _tables(NB, 1)
    t2 = _tables(NB, 2)
    roles = [("s", 0), ("s", 1), ("p", 0), ("p", 1), ("s", 2), ("s", 3), ("p", 2), ("p", 3)]
    in_maps = []
    for c in range(8):
        m = dict(wl)
        kind, k = roles[c]
        if kind == "s":
            m["x"] = np.ascontiguousarray(xs[k])
            m.update(t1)
        else:
            x = np.zeros((NT, D), np.float32)
            x[0:SP] = xp[2 * k]
            x[SP:2 * SP] = xp[2 * k + 1]
            m["x"] = x
            m.update(t2)
        in_maps.append(m)
    res = run_bass_kernel_spmd(nc, in_maps, core_ids=list(range(8)))
    y_s = np.empty((4, SS, D), np.float32)
    y_p = np.empty((8, SP, D), np.float32)
    for c in range(8):
        y = res.results[c]["y"]
        kind, k = roles[c]
        if kind == "s":
            y_s[k] = y
        else:
            y_p[2 * k] = y[0:SP]
            y_p[2 * k + 1] = y[SP:2 * SP]
    return (y_p, y_s), res


def kernel(**inputs):
    out, _ = run(inputs)
    return out
```

```python
import numpy as np
from contextlib import ExitStack
import concourse.bass as bass
import concourse.mybir as mybir
from concourse.bass_utils import run_bass_kernel_spmd

F32 = mybir.dt.float32
BF16 = mybir.dt.bfloat16
I32 = mybir.dt.int32
ALU = mybir.AluOpType
AF = mybir.ActivationFunctionType
D = 1024
EPS = 1e-6


class Obj:
    __slots__ = ("name", "w", "r")

    def __init__(self, name):
        self.name = name
        self.w = None
        self.r = {}


class Sched:
    COMPUTE = ("pe", "act", "dve", "pool")

    def __init__(self, nc, stack):
        self.nc = nc
        self.stack = stack
        self.q = {e: [] for e in ("pe", "act", "dve", "pool", "sp")}
        self.semh = {}
        self.cnt = {}
        self.waited = {e: {} for e in self.q}
        for e in self.COMPUTE:
            self._mksem("c_" + e)
        self.ninst = {e: 0 for e in self.q}

    def _mksem(self, key):
        h = self.stack.enter_context(self.nc.semaphore(key))
        self.semh[key] = h
        self.cnt[key] = 0
        return h

    def dsem(self, key):
        if key not in self.semh:
            self._mksem(key)
        return key

    def _deps(self, eng, reads, writes):
        deps = {}

        def add(k, v):
            if eng == "pe" and k == "c_pe":
                return
            if deps.get(k, 0) < v:
                deps[k] = v
        for o in reads:
            if o.w is not None:
                add(*o.w)
        for o in writes:
            if o.w is not None:
                add(*o.w)
            for k_, v_ in o.r.items():
                add(k_, v_)
        wl = []
        wd = self.waited[eng]
        for k, v in deps.items():
            if wd.get(k, 0) < v:
                wd[k] = v
                wl.append((k, v))
        return wl

    def emit(self, eng, fn, reads=(), writes=(), inc=True, dsem=None):
        wl = self._deps(eng, reads, writes)
        if dsem is not None:
            key = self.dsem(dsem)
            self.cnt[key] += 16
            tok = (key, self.cnt[key])
            incspec = (key, 16)
        else:
            key = "c_" + eng
            if inc:
                self.cnt[key] += 1
                tok = (key, self.cnt[key])
                incspec = (key, 1)
            else:
                tok = (key, self.cnt[key] + 1)
                incspec = None
        self.q[eng].append((wl, fn, incspec))
        self.ninst[eng] += 1
        for o in reads:
            if o.r.get(tok[0], 0) < tok[1]:
                o.r[tok[0]] = tok[1]
        for o in writes:
            o.w = tok
            o.r = {}
        return tok

    def barrier(self):
        for eng in self.q:
            wl = []
            wd = self.waited[eng]
            for k, v in self.cnt.items():
                if v > 0 and wd.get(k, 0) < v:
                    wd[k] = v
                    wl.append((k, v))
            if wl:
                self.q[eng].append((wl, None, None))

    def replay(self):
        nc = self.nc
        semh = self.semh
        q = self.q

        def run(engobj, lst):
            for wl, fn, incspec in lst:
                for k, v in wl:
                    engobj.wait_ge(semh[k], v)
                if fn is None:
                    continue
                ins = fn(engobj)
                if incspec is not None:
                    ins.then_inc(semh[incspec[0]], incspec[1])

        with nc.Block() as block:
            @block.tensor
            def _(e):
                run(e, q["pe"])

            @block.scalar
            def _(e):
                run(e, q["act"])

            @block.vector
            def _(e):
                run(e, q["dve"])

            @block.gpsimd
            def _(e):
                run(e, q["pool"])

            @block.sync
            def _(e):
                run(e, q["sp"])
        for e in q:
            q[e] = []


def rsqrt_dve(S, y, x, tmp, Oy, Ox, Otmp, iters=3):
    xi = x.bitcast(I32)
    yi = y.bitcast(I32)
    S.emit("dve", lambda e: e.tensor_scalar(out=yi, in0=xi, scalar1=1, scalar2=None, op0=ALU.arith_shift_right),
           reads=[Ox], writes=[Oy])
    S.emit("dve", lambda e: e.tensor_scalar(out=yi, in0=yi, scalar1=-1, scalar2=0x5f3759df, op0=ALU.mult, op1=ALU.add),
           reads=[Oy], writes=[Oy])
    for _ in range(iters):
        S.emit("dve", lambda e: e.tensor_tensor(out=tmp, in0=y, in1=y, op=ALU.mult), reads=[Oy], writes=[Otmp])
        S.emit("dve", lambda e: e.tensor_tensor(out=tmp, in0=tmp, in1=x, op=ALU.mult), reads=[Otmp, Ox], writes=[Otmp])
        S.emit("dve", lambda e: e.tensor_scalar(out=tmp, in0=tmp, scalar1=-0.5, scalar2=1.5, op0=ALU.mult, op1=ALU.add),
               reads=[Otmp], writes=[Otmp])
        S.emit("dve", lambda e: e.tensor_tensor(out=y, in0=y, in1=tmp, op=ALU.mult), reads=[Oy, Otmp], writes=[Oy])


def build(NB, debug=False, stop_after=None, lim=99):
    NT = 128 * NB
    NU = NT // 512
    LB = NT // 4
    NB4 = NB // 4
    KP = 2 * NB
    KU = 512 // NB
    R1 = 128 // NB
    KPB = 512 // (2 * NB)
    NBG = max(1, NB // 16)
    BG = min(16, NB)

    nc = bass.Bass("TRN2", target_bir_lowering=False)

    def din(name, shape, dt=F32):
        return nc.dram_tensor(name, shape, dt, kind="ExternalInput").ap()

    def dscr(name, shape, dt):
        if debug and name in ("X1", "X1P", "F1", "F2", "G", "Tscr"):
            return nc.dram_tensor(name, shape, dt, kind="ExternalOutput").ap()
        return nc.dram_tensor(name, shape, dt).ap()

    x_d = din("x", [NT, D])
    w_in0 = din("w_in0", [D, 6144])
    w_out0 = din("w_out0", [2048, D])
    w_in1 = din("w_in1", [D, 5120])
    w_out1 = din("w_out1", [2048, D])
    g0_d = din("g0", [D])
    g1_d = din("g1", [D])
    gf_d = din("gf", [D])
    convw_d = din("convw_l", [128, 24])
    pscale_d = din("pscale_l", [128, 8])
    poolw_d = din("poolw_l", [128, 2048])
    wsT_d = din("wsT_l", [128, 512])
    sgug_d = din("sgug_l", [128, 8])
    bs_d = din("bs_l", [2048])
    fnetw_d = din("fnetw_l", [128, 2048])
    ident_d = din("ident", [128, 128])
    MA_d = din("MA", [128, 512])
    MC_d = din("MC", [KP, 128 * 2 * NB])
    DC_d = din("DC", [128, 1024])
    inv_d = din("inv_tab", [128, 128])
    beta_d = din("beta", [128, 1])
    y_d = nc.dram_tensor("y", [NT, D], F32, kind="ExternalOutput").ap()

    wb_in0 = dscr("wb_in0", [D, 6144], BF16)
    wb_out0 = dscr("wb_out0", [2048, D], BF16)
    wb_in1 = dscr("wb_in1", [D, 5120], BF16)
    wb_out1 = dscr("wb_out1", [2048, D], BF16)
    X1 = dscr("X1", [NT, D], F32)
    X1P = dscr("X1P", [NT, D], F32)
    F1 = dscr("F1", [NT, D], BF16)
    F2 = dscr("F2", [NT, D], BF16)
    G = dscr("G", [NT, D], BF16)
    Tscr = dscr("Tscr", [128, KP * D], BF16)

    with ExitStack() as st:
        S = Sched(nc, st)
        banks = [st.enter_context(nc.psum_tensor("bk%d" % i, [128, 512], F32)) for i in range(8)]
        Ob = [Obj("bk%d" % i) for i in range(8)]
        bstate = [0]

        def nb():
            i = bstate[0]
            bstate[0] = (i + 1) % 8
            return banks[i], Ob[i]

        def sbt(ph, name, shape, dt):
            return ph.enter_context(nc.sbuf_tensor("s_" + name, shape, dt))

        def end_phase():
            S.barrier()
            S.replay()

        with ExitStack() as ph:
            NBUF = 6
            wf = [sbt(ph, "wf%d" % i, [128, 2048], F32) for i in range(NBUF)]
            wbb = [sbt(ph, "wbb%d" % i, [128, 2048], BF16) for i in range(NBUF)]
            Owf = [Obj("wf%d" % i) for i in range(NBUF)]
            Owb = [Obj("wbb%d" % i) for i in range(NBUF)]
            k = 0
            for src, dst, R, C in ((w_in0, wb_in0, D, 6144), (w_out0, wb_out0, 2048, D)):
                for rb in range(R // 128):
                    for c0 in range(0, C, 2048):
                        cw = min(2048, C - c0)
                        b = k % NBUF
                        S.emit("sp", lambda e, b=b, src=src, rb=rb, c0=c0, cw=cw: e.dma_start(
                            out=wf[b][:, 0:cw], in_=src[rb * 128:(rb + 1) * 128, c0:c0 + cw]),
                            writes=[Owf[b]], dsem="wl%d" % b)
                        if k % 2 == 0:
                            S.emit("act", lambda e, b=b, cw=cw: e.activation(out=wbb[b][:, 0:cw], in_=wf[b][:, 0:cw], func=AF.Copy),
                                   reads=[Owf[b]], writes=[Owb[b]])
                        else:
                            S.emit("dve", lambda e, b=b, cw=cw: e.tensor_copy(out=wbb[b][:, 0:cw], in_=wf[b][:, 0:cw]),
                                   reads=[Owf[b]], writes=[Owb[b]])
                        S.emit("pool", lambda e, b=b, dst=dst, rb=rb, c0=c0, cw=cw: e.dma_start(
                            out=dst[rb * 128:(rb + 1) * 128, c0:c0 + cw], in_=wbb[b][:, 0:cw]),
                            reads=[Owb[b]], dsem="ws%d" % b)
                        k += 1
            end_phase()
        if stop_after == "W":
            return nc

        NRING = 5

        with ExitStack() as ph:
            NR0 = 6
            ring = [sbt(ph, "ring%d" % i, [128, 4096], BF16) for i in range(NR0)]
            Oring = [Obj("ring%d" % i) for i in range(NR0)]
            rstate = [0]

            def wload(src_ap_fn):
                i = rstate[0]
                rstate[0] = (i + 1) % NR0
                o, i_ap = src_ap_fn(ring[i])
                S.emit("sp", lambda e, o=o, i_ap=i_ap: e.dma_start(out=o, in_=i_ap), writes=[Oring[i]], dsem="wr%d" % i)
                return ring[i], Oring[i]

            g0r = sbt(ph, "g0r", [128, D], F32)
            convw = sbt(ph, "convw", [128, 24], F32)
            pscale = sbt(ph, "pscale", [128, 8], F32)
            poolw_f = ring[NR0 - 1]
            poolw = sbt(ph, "poolw", [128, 2048], BF16)
            ident_f = sbt(ph, "ident_f", [128, 128], F32)
            ident = sbt(ph, "ident", [128, 128], BF16)
            invt = sbt(ph, "invt", [128, 128], F32)
            beta = sbt(ph, "beta", [128, 1], F32)
            Oc = Obj("consts0")
            for dst_t, src in ((g0r, g0_d.partition_broadcast(128)), (convw, convw_d), (pscale, pscale_d),
                               (ident_f, ident_d), (invt, inv_d), (beta, beta_d)):
                S.emit("sp", lambda e, dst_t=dst_t, src=src: e.dma_start(out=dst_t[:], in_=src), writes=[Oc], dsem="cst")
            S.emit("sp", lambda e: e.dma_start(out=poolw_f[:].bitcast(F32), in_=poolw_d), writes=[Oring[NR0 - 1]], dsem="wr%d" % (NR0 - 1))
            S.emit("dve", lambda e: e.tensor_copy(out=poolw[:], in_=poolw_f[:].bitcast(F32)), reads=[Oring[NR0 - 1], Oc], writes=[Oc])
            S.emit("dve", lambda e: e.tensor_copy(out=ident[:], in_=ident_f[:]), reads=[Oc], writes=[Oc])

            xn = [sbt(ph, "xn0", [128, 4, D], F32)] * 2
            Oxn = [Obj("xn0")] * 2
            junk = sbt(ph, "junk", [128, D], BF16)
            Ojunk = Obj("junk")
            ss = [sbt(ph, "ss%d" % i, [128, 4], F32) for i in range(2)]
            rs = [sbt(ph, "rs%d" % i, [128, 4], F32) for i in range(2)]
            rt_ = [sbt(ph, "rtmp%d" % i, [128, 4], F32) for i in range(2)]
            Oss = [Obj("ss%d" % i) for i in range(2)]
            Ors = [Obj("rs%d" % i) for i in range(2)]
            Ort = [Obj("rt%d" % i) for i in range(2)]
            hb = [sbt(ph, "hb%d" % i, [128, D], BF16) for i in range(4)]
            Ohb = [Obj("hb%d" % i) for i in range(4)]
            hT = [sbt(ph, "hT0", [128, 8, 512], BF16)] * 2
            OhT = [Obj("hT0")] * 2
            Ue = [sbt(ph, "Ue%d" % i, [128, 8, 514], BF16) for i in range(2)]
            OUe = [Obj("Ue%d" % i) for i in range(2)]
            Ve = [sbt(ph, "Ve%d" % i, [128, 8, 528], BF16) for i in range(2)]
            OVe = [Obj("Ve%d" % i) for i in range(2)]
            GAl = [sbt(ph, "GA%d" % i, [128, 8, 512], BF16) for i in range(2)]
            OGAl = [Obj("GA%d" % i) for i in range(2)]
            SZl = [sbt(ph, "SZ%d" % i, [128, 8, 512], BF16) for i in range(2)]
            OSZl = [Obj("SZ%d" % i) for i in range(2)]
            DT = sbt(ph, "DT", [128, 8, 512], BF16)
            ODT = [Obj("DT%d" % i) for i in range(8)]
            YT = sbt(ph, "YT", [128, 16, 512], BF16)
            OYT = [Obj("YT%d" % i) for i in range(16)]
            T1 = [sbt(ph, "T1_%d" % i, [128, 512], F32) for i in range(2)]
            T2 = [sbt(ph, "T2_%d" % i, [128, 512], F32) for i in range(2)]
            OT1 = [Obj("T1_%d" % i) for i in range(2)]
            OT2 = [Obj("T2_%d" % i) for i in range(2)]
            PA = [sbt(ph, "PA0", [128, 528], F32)] * 2
            PB = [sbt(ph, "PB0", [128, 528], F32)] * 2
            OPA = [Obj("PA0")] * 2
            OPB = [Obj("PB0")] * 2
            e8 = sbt(ph, "e8", [128, 8], F32)
            Oe8 = Obj("e8")
            xr = [sbt(ph, "xr%d" % i, [128, D], F32) for i in range(3)]
            Oxr = [Obj("xr%d" % i) for i in range(3)]

            def An(i, src_d, gr, Ogr):
                p = i % 2
                S.emit("sp", lambda e: e.dma_start(
                    out=xn[p][:], in_=src_d[i * 512:(i + 1) * 512, :].rearrange("(r p) d -> p r d", p=128)),
                    writes=[Oxn[p]], dsem="xn%d" % p)
                S.emit("dve", lambda e: e.memset(ss[p][:], 0.0), writes=[Oss[p]])
                for r in range(4):
                    S.emit("act", lambda e, r=r: e.activation(out=junk[:], in_=xn[p][:, r, :], func=AF.Square,
                                                              accum_out=ss[p][:, r:r + 1]),
                           reads=[Oxn[p]], writes=[Ojunk, Oss[p]])
                S.emit("dve", lambda e: e.tensor_scalar(out=ss[p][:], in0=ss[p][:], scalar1=1.0 / D, scalar2=EPS,
                                                        op0=ALU.mult, op1=ALU.add), reads=[Oss[p]], writes=[Oss[p]])
                rsqrt_dve(S, rs[p][:], ss[p][:], rt_[p][:], Ors[p], Oss[p], Ort[p])
                for r in range(4):
                    S.emit("dve", lambda e, r=r: e.scalar_tensor_tensor(
                        out=hb[r][:], in0=xn[p][:, r, :], scalar=rs[p][:, r:r + 1], in1=gr[:], op0=ALU.mult, op1=ALU.mult),
                        reads=[Oxn[p], Ors[p], Ogr], writes=[Ohb[r]])

            def At(i):
                p = i % 2
                for r in range(4):
                    bk, ob = nb()
                    bv = bk[:].bitcast(BF16)
                    for k in range(8):
                        S.emit("pe", lambda e, r=r, k=k, bv=bv: e.transpose(
                            out=bv[:, k * 128:(k + 1) * 128], in_=hb[r][:, k * 128:(k + 1) * 128], identity=ident[:]),
                            reads=[Ohb[r], Oc], writes=[ob], inc=(k == 7))
                    S.emit("act", lambda e, r=r, bv=bv: e.activation(
                        out=hT[p][:, :, r * 128:(r + 1) * 128], in_=bv.rearrange("p (k t) -> p k t", k=8), func=AF.Copy),
                        reads=[ob], writes=[OhT[p]])

            def inproj_fm(i, wsrc, chunk, evac):
                p = i % 2
                slot, oslot = wload(lambda t: (t[:].rearrange("p (k e) -> p k e", k=8),
                                               wsrc[:, chunk * 512:(chunk + 1) * 512].rearrange("(k p) e -> p k e", p=128)))
                sv = slot[:].rearrange("p (k e) -> p k e", k=8)
                for ec in range(4):
                    bk, ob = nb()
                    for k in range(8):
                        S.emit("pe", lambda e, k=k, ec=ec, bk=bk: e.matmul(
                            out=bk[:], lhsT=sv[:, k, ec * 128:(ec + 1) * 128], rhs=hT[p][:, k, :], start=(k == 0), stop=(k == 7)),
                            reads=[oslot, OhT[p]], writes=[ob], inc=(k == 7))
                    evac(bk, ob, chunk * 4 + ec)
                if wsrc is wb_in0:
                    do_casts(1)

            def B1(i):
                p = i % 2

                def ev_ah(bk, ob, ech):
                    ck = ech
                    S.emit("act", lambda e: e.activation(out=Ue[p][:, ck, 1:513], in_=bk[:], func=AF.Copy),
                           reads=[ob], writes=[OUe[p]])

                def ev_ac(bk, ob, ech):
                    ck = ech - 16
                    S.emit("dve", lambda e: e.tensor_tensor(out=Ue[p][:, ck, 1:513], in0=bk[:], in1=Ue[p][:, ck, 1:513], op=ALU.mult),
                           reads=[ob, OUe[p]], writes=[OUe[p]])

                def ev_bv(bk, ob, ech):
                    ck = ech - 32
                    S.emit("act", lambda e: e.activation(out=Ve[p][:, ck, 8:520], in_=bk[:], func=AF.Copy),
                           reads=[ob], writes=[OVe[p]])
                for half in range(2):
                    inproj_fm(i, wb_in0, 0 + half, ev_ah)
                    inproj_fm(i, wb_in0, 4 + half, ev_ac)
                for half in range(2):
                    inproj_fm(i, wb_in0, 8 + half, ev_bv)
                q = 1 - p
                t0 = i * 512
                if i == 0:
                    S.emit("dve", lambda e: e.memset(Ue[p][:, :, 0:1], 0.0), writes=[OUe[p]])
                    S.emit("dve", lambda e: e.memset(Ve[p][:, :, 0:8], 0.0), writes=[OVe[p]])
                else:
                    if t0 % LB == 0:
                        S.emit("dve", lambda e: e.tensor_scalar(out=Ue[p][:, :, 0:1], in0=Ue[q][:, :, 512:513], scalar1=beta[:, 0:1],
                                                                scalar2=None, op0=ALU.mult), reads=[OUe[q], Oc], writes=[OUe[p]])
                        S.emit("dve", lambda e: e.tensor_scalar(out=Ue[q][:, :, 513:514], in0=Ue[p][:, :, 1:2], scalar1=beta[:, 0:1],
                                                                scalar2=None, op0=ALU.mult), reads=[OUe[p], Oc], writes=[OUe[q]])
                        S.emit("dve", lambda e: e.tensor_scalar(out=Ve[p][:, :, 0:8], in0=Ve[q][:, :, 512:520], scalar1=beta[:, 0:1],
                                                                scalar2=None, op0=ALU.mult), reads=[OVe[q], Oc], writes=[OVe[p]])
                        S.emit("dve", lambda e: e.tensor_scalar(out=Ve[q][:, :, 520:528], in0=Ve[p][:, :, 8:16], scalar1=beta[:, 0:1],
                                                                scalar2=None, op0=ALU.mult), reads=[OVe[p], Oc], writes=[OVe[q]])
                    else:
                        S.emit("dve", lambda e: e.tensor_copy(out=Ue[p][:, :, 0:1], in_=Ue[q][:, :, 512:513]), reads=[OUe[q]], writes=[OUe[p]])
                        S.emit("dve", lambda e: e.tensor_copy(out=Ue[q][:, :, 513:514], in_=Ue[p][:, :, 1:2]), reads=[OUe[p]], writes=[OUe[q]])
                        S.emit("dve", lambda e: e.tensor_copy(out=Ve[p][:, :, 0:8], in_=Ve[q][:, :, 512:520]), reads=[OVe[q]], writes=[OVe[p]])
                        S.emit("dve", lambda e: e.tensor_copy(out=Ve[q][:, :, 520:528], in_=Ve[p][:, :, 8:16]), reads=[OVe[p]], writes=[OVe[q]])
                if i == NU - 1:
                    S.emit("dve", lambda e: e.memset(Ue[p][:, :, 513:514], 0.0), writes=[OUe[p]])
                    S.emit("dve", lambda e: e.memset(Ve[p][:, :, 520:528], 0.0), writes=[OVe[p]])

            def B2(i):
                GA, OGA, SZ, OSZ = GAl[i % 2], OGAl[i % 2], SZl[i % 2], OSZl[i % 2]

                def ev_az(bk, ob, ech):
                    ck = ech - 24
                    S.emit("act", lambda e: e.activation(out=GA[:, ck, :], in_=bk[:], func=AF.Silu), reads=[ob], writes=[OGA])

                def ev_ab(bk, ob, ech):
                    ck = ech - 8
                    S.emit("dve", lambda e: e.tensor_tensor(out=GA[:, ck, :], in0=bk[:], in1=GA[:, ck, :], op=ALU.mult),
                           reads=[ob, OGA], writes=[OGA])

                def ev_bz(bk, ob, ech):
                    ck = ech - 40
                    S.emit("act", lambda e: e.activation(out=SZ[:, ck, :], in_=bk[:], func=AF.Silu), reads=[ob], writes=[OSZ])
                for half in range(2):
                    inproj_fm(i, wb_in0, 10 + half, ev_bz)
                for half in range(2):
                    inproj_fm(i, wb_in0, 6 + half, ev_az)
                for half in range(2):
                    inproj_fm(i, wb_in0, 2 + half, ev_ab)

            def S2E(i):
                p = i % 2
                t0 = i * 512
                GA, OGA = GAl[i % 2], OGAl[i % 2]
                for ck in range(8):
                    b = ck % 2
                    S.emit("act", lambda e, ck=ck, b=b: e.activation(out=T1[b][:], in_=Ue[p][:, ck, 0:512], func=AF.Copy,
                                                                     scale=convw[:, ck * 3:ck * 3 + 1]),
                           reads=[OUe[p], Oc], writes=[OT1[b]])
                    S.emit("dve", lambda e, ck=ck, b=b: e.scalar_tensor_tensor(
                        out=T2[b][:], in0=Ue[p][:, ck, 1:513], scalar=convw[:, ck * 3 + 1:ck * 3 + 2], in1=T1[b][:],
                        op0=ALU.mult, op1=ALU.add), reads=[OUe[p], OT1[b], Oc], writes=[OT2[b]])
                    S.emit("dve", lambda e, ck=ck, b=b: e.scalar_tensor_tensor(
                        out=T1[b][:], in0=Ue[p][:, ck, 2:514], scalar=convw[:, ck * 3 + 2:ck * 3 + 3], in1=T2[b][:],
                        op0=ALU.mult, op1=ALU.add), reads=[OUe[p], OT2[b], Oc], writes=[OT1[b]])
                    S.emit("dve", lambda e, ck=ck, b=b: e.tensor_tensor(out=YT[:, ck, :], in0=T1[b][:], in1=GA[:, ck, :], op=ALU.mult),
                           reads=[OT1[b], OGA], writes=[OYT[ck]])
                left_kind = None
                right_kind = None
                if t0 % LB == 0:
                    left_kind = 0 if t0 == 0 else 2
                if (t0 + 512) % LB == 0:
                    right_kind = 1 if (t0 + 512) == NT else 3
                for ck in range(8):
                    g = ck // 2
                    w = 2 << g
                    b = ck % 2
                    v = Ve[p][:, ck, :]
                    S.emit("dve", lambda e, v=v, b=b: e.tensor_tensor(out=PA[b][:, 0:527], in0=v[:, 0:527], in1=v[:, 1:528], op=ALU.add),
                           reads=[OVe[p]], writes=[OPA[b]])
                    cur, Ocur, oth, Ooth = PA[b], OPA[b], PB[b], OPB[b]
                    width = 527
                    for kk in range(1, g + 1):
                        sh = 1 << kk
                        nw = width - sh
                        S.emit("dve", lambda e, cur=cur, oth=oth, nw=nw, sh=sh: e.tensor_tensor(
                            out=oth[:, 0:nw], in0=cur[:, 0:nw], in1=cur[:, sh:sh + nw], op=ALU.add),
                            reads=[Ocur], writes=[Ooth])
                        cur, Ocur, oth, Ooth = oth, Ooth, cur, Ocur
                        width = nw
                    off = 8 - w // 2
                    S.emit("dve", lambda e, cur=cur, off=off, w=w, ck=ck: e.scalar_tensor_tensor(
                        out=DT[:, ck, :], in0=cur[:, off:off + 512], scalar=1.0 / w, in1=Ve[p][:, ck, 8:520],
                        op0=ALU.mult, op1=ALU.subtract), reads=[Ocur, OVe[p]], writes=[ODT[ck]])
                    if left_kind is not None:
                        c0 = (left_kind * 4 + g) * 8
                        S.emit("dve", lambda e, cur=cur, off=off, c0=c0: e.tensor_tensor(
                            out=e8[:], in0=cur[:, off:off + 8], in1=invt[:, c0:c0 + 8], op=ALU.mult),
                            reads=[Ocur, Oc], writes=[Oe8])
                        S.emit("dve", lambda e, ck=ck: e.tensor_tensor(out=DT[:, ck, 0:8], in0=e8[:], in1=Ve[p][:, ck, 8:16], op=ALU.subtract),
                               reads=[Oe8, OVe[p]], writes=[ODT[ck]])
                    if right_kind is not None:
                        c0 = (right_kind * 4 + g) * 8
                        S.emit("dve", lambda e, cur=cur, off=off, c0=c0: e.tensor_tensor(
                            out=e8[:], in0=cur[:, off + 504:off + 512], in1=invt[:, c0:c0 + 8], op=ALU.mult),
                            reads=[Ocur, Oc], writes=[Oe8])
                        S.emit("dve", lambda e, ck=ck: e.tensor_tensor(out=DT[:, ck, 504:512], in0=e8[:], in1=Ve[p][:, ck, 512:520], op=ALU.subtract),
                               reads=[Oe8, OVe[p]], writes=[ODT[ck]])

            def S2M(i):
                p = i % 2
                t0 = i * 512
                do_casts(max(0, per_unit - 12))
                SZ, OSZ = SZl[i % 2], OSZl[i % 2]
                pwv = poolw[:].rearrange("p (g k j) -> p g k j", g=4, k=2)
                for g in range(4):
                    for jc in range(2):
                        bk, ob = nb()
                        for k2 in range(2):
                            S.emit("pe", lambda e, g=g, jc=jc, k2=k2, bk=bk: e.matmul(
                                out=bk[:], lhsT=pwv[:, g, k2, jc * 128:(jc + 1) * 128], rhs=DT[:, 2 * g + k2, :],
                                start=(k2 == 0), stop=(k2 == 1)),
                                reads=[Oc, ODT[2 * g + k2]], writes=[ob], inc=(k2 == 1))
                        ck = 2 * g + jc
                        S.emit("dve", lambda e, ck=ck, bk=bk: e.scalar_tensor_tensor(
                            out=YT[:, 8 + ck, :], in0=bk[:], scalar=pscale[:, ck:ck + 1], in1=SZ[:, ck, :],
                            op0=ALU.mult, op1=ALU.mult), reads=[ob, OSZ, Oc], writes=[OYT[8 + ck]])
                slots = []
                for qk in range(4):
                    slots.append(wload(lambda t, qk=qk: (t[:].rearrange("p (k d) -> p k d", k=4),
                                                         wb_out0[qk * 512:(qk + 1) * 512, :].rearrange("(k p) d -> p k d", p=128))))
                for r in range(4):
                    xb = (i * 4 + r) % 3
                    S.emit("sp", lambda e, r=r, xb=xb: e.dma_start(out=xr[xb][:], in_=x_d[t0 + r * 128:t0 + (r + 1) * 128, :]),
                           writes=[Oxr[xb]], dsem="xr%d" % xb)
                    for hf in range(2):
                        bk, ob = nb()
                        for ek in range(16):
                            slot, oslot = slots[ek // 4]
                            sv = slot[:].rearrange("p (k d) -> p k d", k=4)
                            S.emit("pe", lambda e, r=r, hf=hf, ek=ek, sv=sv, bk=bk: e.matmul(
                                out=bk[:], lhsT=YT[:, ek, r * 128:(r + 1) * 128], rhs=sv[:, ek % 4, hf * 512:(hf + 1) * 512],
                                start=(ek == 0), stop=(ek == 15)),
                                reads=[oslot, OYT[ek]], writes=[ob], inc=(ek == 15))
                        S.emit("dve", lambda e, hf=hf, xb=xb, bk=bk: e.tensor_tensor(
                            out=xr[xb][:, hf * 512:(hf + 1) * 512], in0=bk[:], in1=xr[xb][:, hf * 512:(hf + 1) * 512], op=ALU.add),
                            reads=[ob, Oxr[xb]], writes=[Oxr[xb]])
                    S.emit("pool", lambda e, r=r, xb=xb: e.dma_start(out=X1[t0 + r * 128:t0 + (r + 1) * 128, :], in_=xr[xb][:]),
                           reads=[Oxr[xb]], dsem="x1st%d" % xb)

            wc_f = [sbt(ph, "wc_f%d" % i, [128, 256], F32) for i in range(2)]
            wc_b = [sbt(ph, "wc_b%d" % i, [128, 256], BF16) for i in range(2)]
            Owc_f = [Obj("wc_f%d" % i) for i in range(2)]
            Owc_b = [Obj("wc_b%d" % i) for i in range(2)]
            cast_jobs = []
            for src, dst, R, C in ((w_in1, wb_in1, D, 5120), (w_out1, wb_out1, 2048, D)):
                for rb in range(R // 128):
                    for c0 in range(0, C, 256):
                        cast_jobs.append((src, dst, rb, c0))
            cj = [0]

            def cast_load(j):
                if j >= len(cast_jobs):
                    return
                src, dst, rb, c0 = cast_jobs[j]
                b = j % 2
                S.emit("pool", lambda e: e.dma_start(out=wc_f[b][:], in_=src[rb * 128:(rb + 1) * 128, c0:c0 + 256]),
                       writes=[Owc_f[b]], dsem="wcl%d" % b)

            def do_casts(n):
                for _ in range(n):
                    j = cj[0]
                    if j >= len(cast_jobs):
                        return
                    if j == 0:
                        cast_load(0)
                    cast_load(j + 1)
                    src, dst, rb, c0 = cast_jobs[j]
                    b = j % 2
                    cj[0] += 1
                    S.emit("pool", lambda e, b=b: e.tensor_copy(out=wc_b[b][:], in_=wc_f[b][:]),
                           reads=[Owc_f[b]], writes=[Owc_b[b]])
                    S.emit("pool", lambda e, b=b, dst=dst, rb=rb, c0=c0: e.dma_start(
                        out=dst[rb * 128:(rb + 1) * 128, c0:c0 + 256], in_=wc_b[b][:]), reads=[Owc_b[b]], dsem="wcs%d" % b)
            per_unit = (len(cast_jobs) + NU - 1) // NU

            An(0, x_d, g0r, Oc)
            At(0)
            for i in range(NU):
                if i + 1 < NU:
                    An(i + 1, x_d, g0r, Oc)
                B1(i)
                if i >= 1:
                    S2E(i - 1)
                B2(i)
                if i + 1 < NU:
                    At(i + 1)
                if i >= 1:
                    S2M(i - 1)
            S2E(NU - 1)
            S2M(NU - 1)
            end_phase()
        if stop_after == "L0":
            return nc

        with ExitStack() as ph:
            ring = [sbt(ph, "ringb%d" % i, [128, 4096], BF16) for i in range(NRING)]
            Oring = [Obj("ringb%d" % i) for i in range(NRING)]
            rstate = [0]

            def wload(src_ap_fn):
                i = rstate[0]
                rstate[0] = (i + 1) % NRING
                o, i_ap = src_ap_fn(ring[i])
                S.emit("sp", lambda e, o=o, i_ap=i_ap: e.dma_start(out=o, in_=i_ap), writes=[Oring[i]], dsem="wr%d" % i)
                return ring[i], Oring[i]

            g1r = sbt(ph, "g1r", [128, D], F32)
            ident_f = sbt(ph, "ident_fb", [128, 128], F32)
            ident = sbt(ph, "identb", [128, 128], BF16)
            wsT_f = sbt(ph, "wsT_f", [128, 512], F32)
            wsT = sbt(ph, "wsT", [128, 512], BF16)
            sgug = sbt(ph, "sgug", [128, 8], F32)
            bsr = sbt(ph, "bsr", [128, 2048], F32)
            Wc = sbt(ph, "Wc", [128, 8, D], BF16)
            Oc = Obj("consts1")
            for dst_ap, src in ((g1r[:], g1_d.partition_broadcast(128)), (ident_f[:], ident_d), (wsT_f[:], wsT_d),
                                (sgug[:], sgug_d), (bsr[:], bs_d.partition_broadcast(128)),
                                (Wc[:], wb_out1[0:1024, :].rearrange("(k p) d -> p k d", p=128))):
                S.emit("sp", lambda e, dst_ap=dst_ap, src=src: e.dma_start(out=dst_ap, in_=src), writes=[Oc], dsem="cst")
            S.emit("dve", lambda e: e.tensor_copy(out=wsT[:], in_=wsT_f[:]), reads=[Oc], writes=[Oc])
            S.emit("dve", lambda e: e.tensor_copy(out=ident[:], in_=ident_f[:]), reads=[Oc], writes=[Oc])

            xn = [sbt(ph, "xnb%d" % i, [128, 4, D], F32) for i in range(2)]
            Oxn = [Obj("xnb%d" % i) for i in range(2)]
            junk = sbt(ph, "junkb", [128, D], BF16)
            Ojunk = Obj("junkb")
            ss = [sbt(ph, "ssb%d" % i, [128, 4], F32) for i in range(2)]
            rs = [sbt(ph, "rsb%d" % i, [128, 4], F32) for i in range(2)]
            rt_ = [sbt(ph, "rtmpb%d" % i, [128, 4], F32) for i in range(2)]
            Oss = [Obj("ssb%d" % i) for i in range(2)]
            Ors = [Obj("rsb%d" % i) for i in range(2)]
            Ort = [Obj("rtb%d" % i) for i in range(2)]
            hb = [sbt(ph, "hbb%d" % i, [128, D], BF16) for i in range(4)]
            Ohb = [Obj("hbb%d" % i) for i in range(4)]
            hT = [sbt(ph, "hTb%d" % i, [128, 8, 512], BF16) for i in range(2)]
            OhT = [Obj("hTb%d" % i) for i in range(2)]
            GC = sbt(ph, "GC", [128, 8, 512], BF16)
            OGC = Obj("GC")
            VR = sbt(ph, "VR", [128, 4, D], F32)
            OVR = [Obj("VR%d" % i) for i in range(4)]
            VN = sbt(ph, "VN", [128, 4, D], BF16)
            OVN = [Obj("VN%d" % i) for i in range(4)]
            st6 = sbt(ph, "st6", [128, 16, 6], F32)
            Ost6 = Obj("st6")
            mv = sbt(ph, "mv", [128, 16, 2], F32)
            Omv = Obj("mv")
            sums = sbt(ph, "sums", [128, 16], F32)
            sumsq = sbt(ph, "sumsq", [128, 16], F32)
            mean16 = sbt(ph, "mean16", [128, 16], F32)
            msq16 = sbt(ph, "msq16", [128, 16], F32)
            Osums, Osumsq, Omean16, Omsq16 = Obj("sums"), Obj("sumsq"), Obj("mean16"), Obj("msq16")
            var_e = sbt(ph, "var_e", [128, 16], F32)
            rstd16 = sbt(ph, "rstd16", [128, 16], F32)
            rtmp16 = sbt(ph, "rtmp16", [128, 16], F32)
            Ovar = Obj("var_e")
            Orstd16 = Obj("rstd16")
            Ortmp16 = Obj("rtmp16")
            FS = sbt(ph, "FS", [128, 4, D], BF16)
            OFS = Obj("FS")
            GS = sbt(ph, "GS", [128, 4, D], BF16)
            OGS = Obj("GS")
            TM = [sbt(ph, "TM%d" % i, [128, 512], F32) for i in range(2)]
            OTM = [Obj("TM%d" % i) for i in range(2)]
            YC = sbt(ph, "YC", [128, 8, 512], BF16)
            OYC = [Obj("YC%d" % i) for i in range(8)]
            x1ps = [sbt(ph, "x1ps%d" % i, [128, D], F32) for i in range(2)]
            Ox1ps = [Obj("x1ps%d" % i) for i in range(2)]

            def An1(i):
                p = i % 2
                S.emit("sp", lambda e: e.dma_start(
                    out=xn[p][:], in_=X1[i * 512:(i + 1) * 512, :].rearrange("(r p) d -> p r d", p=128)),
                    writes=[Oxn[p]], dsem="xn%d" % p)
                S.emit("dve", lambda e: e.memset(ss[p][:], 0.0), writes=[Oss[p]])
                for r in range(4):
                    S.emit("act", lambda e, r=r: e.activation(out=junk[:], in_=xn[p][:, r, :], func=AF.Square,
                                                              accum_out=ss[p][:, r:r + 1]),
                           reads=[Oxn[p]], writes=[Ojunk, Oss[p]])
                S.emit("dve", lambda e: e.tensor_scalar(out=ss[p][:], in0=ss[p][:], scalar1=1.0 / D, scalar2=EPS,
                                                        op0=ALU.mult, op1=ALU.add), reads=[Oss[p]], writes=[Oss[p]])
                rsqrt_dve(S, rs[p][:], ss[p][:], rt_[p][:], Ors[p], Oss[p], Ort[p])
                for r in range(4):
                    S.emit("dve", lambda e, r=r: e.scalar_tensor_tensor(
                        out=hb[r][:], in0=xn[p][:, r, :], scalar=rs[p][:, r:r + 1], in1=g1r[:], op0=ALU.mult, op1=ALU.mult),
                        reads=[Oxn[p], Ors[p], Oc], writes=[Ohb[r]])

            def At1(i):
                p = i % 2
                for r in range(4):
                    bk, ob = nb()
                    bv = bk[:].bitcast(BF16)
                    for k in range(8):
                        S.emit("pe", lambda e, r=r, k=k, bv=bv: e.transpose(
                            out=bv[:, k * 128:(k + 1) * 128], in_=hb[r][:, k * 128:(k + 1) * 128], identity=ident[:]),
                            reads=[Ohb[r], Oc], writes=[ob], inc=(k == 7))
                    S.emit("act", lambda e, r=r, bv=bv: e.activation(
                        out=hT[p][:, :, r * 128:(r + 1) * 128], in_=bv.rearrange("p (k t) -> p k t", k=8), func=AF.Copy),
                        reads=[ob], writes=[OhT[p]])

            def inproj_fm1(i, chunk, evac):
                p = i % 2
                slot, oslot = wload(lambda t: (t[:].rearrange("p (k e) -> p k e", k=8),
                                               wb_in1[:, chunk * 512:(chunk + 1) * 512].rearrange("(k p) e -> p k e", p=128)))
                sv = slot[:].rearrange("p (k e) -> p k e", k=8)
                for ec in range(4):
                    bk, ob = nb()
                    for k in range(8):
                        S.emit("pe", lambda e, k=k, ec=ec, bk=bk: e.matmul(
                            out=bk[:], lhsT=sv[:, k, ec * 128:(ec + 1) * 128], rhs=hT[p][:, k, :], start=(k == 0), stop=(k == 7)),
                            reads=[oslot, OhT[p]], writes=[ob], inc=(k == 7))
                    evac(bk, ob, chunk * 4 + ec)

            def inproj_tm1(i, chunk, evac, after=None):
                p = i % 2
                slot, oslot = wload(lambda t: (t[:].rearrange("p (k e) -> p k e", k=8),
                                               wb_in1[:, chunk * 512:(chunk + 1) * 512].rearrange("(k p) e -> p k e", p=128)))
                sv = slot[:].rearrange("p (k e) -> p k e", k=8)
                for r in range(4):
                    bk, ob = nb()
                    for k in range(8):
                        S.emit("pe", lambda e, k=k, r=r, bk=bk: e.matmul(
                            out=bk[:], lhsT=hT[p][:, k, r * 128:(r + 1) * 128], rhs=sv[:, k, :], start=(k == 0), stop=(k == 7)),
                            reads=[oslot, OhT[p]], writes=[ob], inc=(k == 7))
                    evac(bk, ob, r)
                    if after is not None:
                        after(r)

            def L1A(i):
                t0 = i * 512
                p = i % 2

                def ev_cz(bk, ob, ech):
                    ck = ech - 16
                    S.emit("act", lambda e: e.activation(out=GC[:, ck, :], in_=bk[:], func=AF.Silu), reads=[ob], writes=[OGC])

                def ev_cu(bk, ob, ech):
                    ck = ech
                    S.emit("dve", lambda e: e.tensor_tensor(out=GC[:, ck, :], in0=bk[:], in1=GC[:, ck, :], op=ALU.mult),
                           reads=[ob, OGC], writes=[OGC])
                for half in range(2):
                    inproj_fm1(i, 4 + half, ev_cz)
                    inproj_fm1(i, 0 + half, ev_cu)
                if i + 1 < NU:
                    An1(i + 1)
                if lim < 0.3:
                    return
                S.emit("dve", lambda e: e.memset(sums[:], 0.0), writes=[Osums])
                S.emit("dve", lambda e: e.memset(sumsq[:], 0.0), writes=[Osumsq])
                for half in range(2):
                    def ev_cv(bk, ob, r, half=half):
                        for hh in range(2):
                            idx = r * 4 + half * 2 + hh
                            c0 = (half * 2 + hh) * 256
                            S.emit("act", lambda e, hh=hh, idx=idx, c0=c0: e.activation(
                                out=VR[:, r, c0:c0 + 256], in_=bk[:, hh * 256:(hh + 1) * 256], func=AF.Copy,
                                accum_out=sums[:, idx:idx + 1]), reads=[ob], writes=[OVR[r], Osums])
                            S.emit("act", lambda e, hh=hh, idx=idx: e.activation(
                                out=junk[:, 0:256], in_=bk[:, hh * 256:(hh + 1) * 256], func=AF.Square,
                                accum_out=sumsq[:, idx:idx + 1]), reads=[ob], writes=[Ojunk, Osumsq])
                    inproj_tm1(i, 2 + half, ev_cv)
                if lim < 0.7:
                    return
                S.emit("dve", lambda e: e.tensor_scalar(out=mean16[:], in0=sums[:], scalar1=1.0 / 256, scalar2=None, op0=ALU.mult),
                       reads=[Osums], writes=[Omean16])
                S.emit("dve", lambda e: e.tensor_tensor(out=msq16[:], in0=mean16[:], in1=mean16[:], op=ALU.mult),
                       reads=[Omean16], writes=[Omsq16])
                S.emit("dve", lambda e: e.scalar_tensor_tensor(out=var_e[:], in0=sumsq[:], scalar=1.0 / 256, in1=msq16[:],
                                                               op0=ALU.mult, op1=ALU.subtract), reads=[Osumsq, Omsq16], writes=[Ovar])
                S.emit("dve", lambda e: e.tensor_scalar(out=var_e[:], in0=var_e[:], scalar1=EPS, scalar2=None, op0=ALU.add),
                       reads=[Ovar], writes=[Ovar])
                rsqrt_dve(S, rstd16[:], var_e[:], rtmp16[:], Orstd16, Ovar, Ortmp16)
                if lim < 0.9:
                    return
                for r in range(4):
                    for h in range(4):
                        idx = r * 4 + h
                        S.emit("dve", lambda e, r=r, h=h, idx=idx: e.tensor_scalar(
                            out=VN[:, r, h * 256:(h + 1) * 256], in0=VR[:, r, h * 256:(h + 1) * 256],
                            scalar1=mean16[:, idx:idx + 1], scalar2=rstd16[:, idx:idx + 1], op0=ALU.subtract, op1=ALU.mult),
                            reads=[OVR[r], Omean16, Orstd16], writes=[OVN[r]])
                if lim < 2:
                    return
                for half in range(2):
                    def ev_df(bk, ob, r, half=half):
                        S.emit("act", lambda e: e.activation(out=FS[:, r, half * 512:(half + 1) * 512], in_=bk[:], func=AF.Copy),
                               reads=[ob], writes=[OFS])
                    inproj_tm1(i, 6 + half, ev_df)
                S.emit("pool", lambda e: e.dma_start(out=F1[t0:t0 + 512, :].rearrange("(r p) d -> p r d", p=128), in_=FS[:]),
                       reads=[OFS], dsem="f1st")
                for r in range(4):
                    t = t0 + r * 128
                    bhi = t // LB
                    rr = t % LB
                    a0 = rr // NB4
                    row0 = NB * a0 + NB4 * bhi
                    na = 128 // NB4
                    dst = bass.AP(F2.tensor, row0 * D, [[NB * D, na], [D, NB4], [1, D]])
                    S.emit("pool", lambda e, r=r, dst=dst: e.dma_start(out=dst, in_=FS[:, r, :]), reads=[OFS], dsem="f2st")
                if lim < 4:
                    return
                wsv = wsT[:].rearrange("p (h q) -> p h q", h=4)
                bsv = bsr[:].rearrange("p (h c) -> p h c", h=4)

                def sgu_ck(ck):
                    h = ck // 2
                    b = ck % 2
                    bk, ob = nb()
                    for r in range(4):
                        S.emit("pe", lambda e, r=r, bk=bk: e.matmul(
                            out=bk[:, r * 128:(r + 1) * 128], lhsT=VN[:, r, ck * 128:(ck + 1) * 128], rhs=wsv[:, h, :],
                            start=True, stop=True), reads=[OVN[r], Oc], writes=[ob], inc=(r == 3))
                    S.emit("dve", lambda e, bk=bk: e.scalar_tensor_tensor(
                        out=TM[b][:], in0=bk[:], scalar=sgug[:, ck:ck + 1], in1=bsv[:, h, :], op0=ALU.mult, op1=ALU.add),
                        reads=[ob, Oc], writes=[OTM[b]])
                    S.emit("dve", lambda e: e.tensor_tensor(out=YC[:, ck, :], in0=TM[b][:], in1=GC[:, ck, :], op=ALU.mult),
                           reads=[OTM[b], OGC], writes=[OYC[ck]])
                for half in range(2):
                    def ev_dz(bk, ob, r, half=half):
                        S.emit("act", lambda e: e.activation(out=GS[:, r, half * 512:(half + 1) * 512], in_=bk[:], func=AF.Silu),
                               reads=[ob], writes=[OGS])
                    inproj_tm1(i, 8 + half, ev_dz, after=(None if half == 0 else (lambda r: (sgu_ck(2 * r), sgu_ck(2 * r + 1)))))
                S.emit("pool", lambda e: e.dma_start(out=G[t0:t0 + 512, :].rearrange("(r p) d -> p r d", p=128), in_=GS[:]),
                       reads=[OGS], dsem="gst")
                if i + 1 < NU:
                    At1(i + 1)
                if lim < 5:
                    return
                for r in range(4):
                    xb = (i * 4 + r) % 2
                    for hf in range(2):
                        bk, ob = nb()
                        for ek in range(8):
                            S.emit("pe", lambda e, r=r, hf=hf, ek=ek, bk=bk: e.matmul(
                                out=bk[:], lhsT=YC[:, ek, r * 128:(r + 1) * 128], rhs=Wc[:, ek, hf * 512:(hf + 1) * 512],
                                start=(ek == 0), stop=(ek == 7)), reads=[Oc, OYC[ek]], writes=[ob], inc=(ek == 7))
                        S.emit("dve", lambda e, r=r, hf=hf, xb=xb, bk=bk: e.tensor_tensor(
                            out=x1ps[xb][:, hf * 512:(hf + 1) * 512], in0=bk[:], in1=xn[p][:, r, hf * 512:(hf + 1) * 512], op=ALU.add),
                            reads=[ob, Oxn[p]], writes=[Ox1ps[xb]])
                    S.emit("pool", lambda e, r=r, xb=xb: e.dma_start(out=X1P[t0 + r * 128:t0 + (r + 1) * 128, :], in_=x1ps[xb][:]),
                           reads=[Ox1ps[xb]], dsem="x1pst%d" % xb)

            An1(0)
            At1(0)
            for i in range(NU):
                L1A(i)
            end_phase()
        if stop_after == "L1A":
            return nc

        gfr = sbt(st, "gfr", [128, D], F32)
        ident_f = sbt(st, "ident_fc", [128, 128], F32)
        ident = sbt(st, "identc", [128, 128], BF16)
        DCt = sbt(st, "DCt", [128, 1024], F32)
        fw = sbt(st, "fw", [128, 2048], F32)
        M2 = sbt(st, "M2", [128, 4, 4, 256], BF16)
        Wd = sbt(st, "Wd", [128, 8, D], BF16)
        OcC = Obj("constsC")
        for dst_ap, src in ((gfr[:], gf_d.partition_broadcast(128)), (ident_f[:], ident_d), (DCt[:], DC_d), (fw[:], fnetw_d),
                            (Wd[:], wb_out1[1024:2048, :].rearrange("(k p) d -> p k d", p=128))):
            S.emit("sp", lambda e, dst_ap=dst_ap, src=src: e.dma_start(out=dst_ap, in_=src), writes=[OcC], dsem="cst")
        S.emit("dve", lambda e: e.tensor_copy(out=ident[:], in_=ident_f[:]), reads=[OcC], writes=[OcC])
        DCv = DCt[:].rearrange("p (k m c) -> p k m c", k=2, m=2)
        fwv = fw[:].rearrange("p (g k j) -> p g k j", g=4, k=2)
        for g in range(4):
            for pq in range(2):
                for c2 in range(2):
                    bk, ob = nb()
                    for k in range(2):
                        S.emit("pe", lambda e, g=g, pq=pq, c2=c2, k=k, bk=bk: e.matmul(
                            out=bk[:, 0:256], lhsT=DCv[:, k, pq, c2 * 128:(c2 + 1) * 128], rhs=fwv[:, g, k, :],
                            start=(k == 0), stop=(k == 1)), reads=[OcC], writes=[ob], inc=(k == 1))
                    S.emit("act", lambda e, g=g, pq=pq, c2=c2, bk=bk: e.activation(
                        out=M2[:, g, pq * 2 + c2, :], in_=bk[:, 0:256], func=AF.Copy), reads=[ob], writes=[OcC])


        with ExitStack() as ph:
            MA_f = sbt(ph, "MA_f", [128, 512], F32)
            MAb = sbt(ph, "MAb", [128, 512], BF16)
            Oc = Obj("constsA")
            S.emit("sp", lambda e: e.dma_start(out=MA_f[:], in_=MA_d), writes=[Oc], dsem="cst")
            S.emit("dve", lambda e: e.tensor_copy(out=MAb[:], in_=MA_f[:]), reads=[Oc], writes=[Oc])
            MAv = MAb[:].rearrange("p (m k) -> p m k", m=4)
            F1t = [sbt(ph, "F1t%d" % i, [128, BG, 512], BF16) for i in range(2)]
            F2t = [sbt(ph, "F2t%d" % i, [128, BG, 512], BF16) for i in range(2)]
            OF1t = [Obj("F1t%d" % i) for i in range(2)]
            OF2t = [Obj("F2t%d" % i) for i in range(2)]
            TSB = 4
            TS = [sbt(ph, "TS%d" % i, [128, 2, TSB, 512], BF16) for i in range(3)]
            OTS = [Obj("TS%d" % i) for i in range(3)]
            F1v = F1.rearrange("(a b) c -> a b c", b=NB)
            F2v = F2.rearrange("(a b) c -> a b c", b=NB)
            Tv = Tscr.rearrange("k (m b c) -> k m b c", m=2, b=NB)
            n = 0
            tsn = 0
            for cb in range(2):
                for bg in range(NBG):
                    p = n % 2
                    n += 1
                    S.emit("sp", lambda e, p=p, bg=bg, cb=cb: e.dma_start(
                        out=F1t[p][:], in_=F1v[:, bg * BG:(bg + 1) * BG, cb * 512:(cb + 1) * 512]), writes=[OF1t[p]], dsem="f1l%d" % p)
                    S.emit("sp", lambda e, p=p, bg=bg, cb=cb: e.dma_start(
                        out=F2t[p][:], in_=F2v[:, bg * BG:(bg + 1) * BG, cb * 512:(cb + 1) * 512]), writes=[OF2t[p]], dsem="f2l%d" % p)
                    for bl in range(BG):
                        b_abs = bg * BG + bl
                        tb = (tsn // TSB) % 3
                        bi = tsn % TSB
                        tsn += 1
                        for m in range(2):
                            bk, ob = nb()
                            S.emit("pe", lambda e, m=m, p=p, bl=bl, bk=bk: e.matmul(
                                out=bk[:], lhsT=MAv[:, m, :], rhs=F1t[p][:, bl, :], start=True, stop=False),
                                reads=[Oc, OF1t[p]], writes=[ob], inc=False)
                            S.emit("pe", lambda e, m=m, p=p, bl=bl, bk=bk: e.matmul(
                                out=bk[:], lhsT=MAv[:, 2 + m, :], rhs=F2t[p][:, bl, :], start=False, stop=True),
                                reads=[Oc, OF2t[p]], writes=[ob], inc=True)
                            if m == 0:
                                S.emit("act", lambda e, tb=tb, bi=bi, bk=bk: e.activation(out=TS[tb][:, 0, bi, :], in_=bk[:], func=AF.Copy),
                                       reads=[ob], writes=[OTS[tb]])
                            else:
                                S.emit("dve", lambda e, tb=tb, bi=bi, bk=bk: e.tensor_copy(out=TS[tb][:, 1, bi, :], in_=bk[:]),
                                       reads=[ob], writes=[OTS[tb]])
                        if bi == TSB - 1:
                            b0 = b_abs - (TSB - 1)
                            S.emit("pool", lambda e, tb=tb, b0=b0, cb=cb: e.dma_start(
                                out=Tv[:, :, b0:b0 + TSB, cb * 512:(cb + 1) * 512], in_=TS[tb][:]), reads=[OTS[tb]], dsem="tst%d" % tb)
            end_phase()
        if stop_after == "FA":
            return nc

        with ExitStack() as ph:
            Oc = OcC
            if KU <= 4:
                Tt = [sbt(ph, "Tt%d" % i, [KP, KU, D], BF16) for i in range(2)]
                OTt = [Obj("Tt%d" % i) for i in range(2)]
            else:
                Tt = [sbt(ph, "Tt0", [KP, KU, D], BF16)] * 2
                OTt = [Obj("Tt0")] * 2
            MCf = [sbt(ph, "MCf%d" % i, [KP, KU * 2 * NB], F32) for i in range(3)]
            MCb = [sbt(ph, "MCb%d" % i, [KP, KU * 2 * NB], BF16) for i in range(3)]
            OMCf = [Obj("MCf%d" % i) for i in range(3)]
            OMCb = [Obj("MCb%d" % i) for i in range(3)]
            PQT = sbt(ph, "PQT", [128, 8, 2, 512], BF16)
            OPQ = [Obj("PQ%d" % i) for i in range(8)]
            Gr = [sbt(ph, "Gr%d" % i, [128, D], BF16) for i in range(4)]
            OGr = [Obj("Gr%d" % i) for i in range(4)]
            GZT = sbt(ph, "GZT", [128, 8, 512], BF16)
            OGZT = Obj("GZT")
            YDT = sbt(ph, "YDT", [128, 8, 512], BF16)
            OYD = [Obj("YD%d" % i) for i in range(8)]
            NX1R = 4 if KU <= 8 else 2
            x1r = [sbt(ph, "x1r%d" % i, [128, D], F32) for i in range(NX1R)]
            Ox1r = [Obj("x1r%d" % i) for i in range(NX1R)]
            X2 = sbt(ph, "X2", [128, 4, D], F32)
            OX2 = [Obj("X2_%d" % i) for i in range(4)]
            junk = sbt(ph, "junkc", [128, D], BF16)
            Ojunk = Obj("junkc")
            ssf = sbt(ph, "ssf", [128, 4], F32)
            rsf = sbt(ph, "rsf", [128, 4], F32)
            rtf = sbt(ph, "rtf", [128, 4], F32)
            Ossf, Orsf, Ortf = Obj("ssf"), Obj("rsf"), Obj("rtf")
            outs = [sbt(ph, "outs%d" % i, [128, D], F32) for i in range(4)]
            Oouts = [Obj("outs%d" % i) for i in range(4)]
            Tsv = Tscr.rearrange("k (q c) -> q k c", c=D)
            MCv = MC_d.rearrange("q (k n) -> q k n", n=2 * NB)

            K2H = NB // 2

            def rows_ap(tensor_ap, u, r):
                kq = r // 2
                k2_0 = (r % 2) * K2H
                base = u * KU + kq * KPB + 128 * k2_0
                return bass.AP(tensor_ap.tensor, base * D, [[128 * D, K2H], [D, KPB], [1, D]])

            def fc_L(u):
                p = u % 2
                S.emit("sp", lambda e: e.dma_start(out=Tt[p][:], in_=Tsv[:, u * KU:(u + 1) * KU, :]),
                       writes=[OTt[p]], dsem="ttl%d" % p)

            def fc_MCl(u):
                p = u % 3
                S.emit("sp", lambda e: e.dma_start(out=MCf[p][:].rearrange("q (k n) -> q k n", n=2 * NB),
                                                   in_=MCv[:, u * KU:(u + 1) * KU, :]), writes=[OMCf[p]], dsem="mcl%d" % p)

            def fc_MCc(u):
                p = u % 3
                S.emit("dve", lambda e: e.tensor_copy(out=MCb[p][:], in_=MCf[p][:]), reads=[OMCf[p]], writes=[OMCb[p]])

            def fc_Gl(u):
                for r in range(4):
                    gb = r
                    S.emit("sp", lambda e, gb=gb, r=r: e.dma_start(out=Gr[gb][:], in_=rows_ap(G, u, r)),
                           writes=[OGr[gb]], dsem="grl%d" % gb)

            def fc_Gt(u):
                for r in range(4):
                    gb = r
                    bk, ob = nb()
                    bv = bk[:].bitcast(BF16)
                    for k in range(8):
                        S.emit("pe", lambda e, gb=gb, k=k, bv=bv: e.transpose(
                            out=bv[:, k * 128:(k + 1) * 128], in_=Gr[gb][:, k * 128:(k + 1) * 128], identity=ident[:]),
                            reads=[OGr[gb], Oc], writes=[ob], inc=(k == 7))
                    S.emit("act", lambda e, r=r, bv=bv: e.activation(
                        out=GZT[:, :, r * 128:(r + 1) * 128], in_=bv.rearrange("p (k t) -> p k t", k=8), func=AF.Copy),
                        reads=[ob], writes=[OGZT])

            def fc_C(u):
                p = u % 2
                pm = u % 3
                MCbv = MCb[pm][:].rearrange("q (k n) -> q k n", n=2 * NB)
                for ck in range(8):
                    for kq in range(KU // KPB):
                        bk, ob = nb()
                        bkv = bk[:].rearrange("p (q n k) -> p k q n", q=2, n=NB, k=KPB)
                        for kk in range(KPB):
                            k1l = kq * KPB + kk
                            S.emit("pe", lambda e, ck=ck, kk=kk, k1l=k1l, bkv=bkv: e.matmul(
                                out=bkv[:, kk, :, :], lhsT=Tt[p][:, k1l, ck * 128:(ck + 1) * 128],
                                rhs=MCbv[:, k1l, :], start=True, stop=True),
                                reads=[OTt[p], OMCb[pm]], writes=[ob], inc=(kk == KPB - 1))
                        lo = kq * KPB * NB
                        hi = (kq + 1) * KPB * NB
                        outv = PQT[:, ck, :, lo:hi]
                        inv_ = bk[:].rearrange("p (q m) -> p q m", q=2)
                        S.emit("act", lambda e, outv=outv, inv_=inv_: e.activation(out=outv, in_=inv_, func=AF.Copy),
                               reads=[ob], writes=[OPQ[ck]])

            def fc_D(u):
                for g in range(4):
                    for jc in range(2):
                        bk, ob = nb()
                        n_ = 0
                        for pq in range(2):
                            for c2 in range(2):
                                S.emit("pe", lambda e, g=g, jc=jc, pq=pq, c2=c2, bk=bk, n_=n_: e.matmul(
                                    out=bk[:], lhsT=M2[:, g, pq * 2 + c2, jc * 128:(jc + 1) * 128], rhs=PQT[:, 2 * g + c2, pq, :],
                                    start=(n_ == 0), stop=(n_ == 3)), reads=[Oc, OPQ[2 * g + c2]], writes=[ob], inc=(n_ == 3))
                                n_ += 1
                        ck = 2 * g + jc
                        S.emit("dve", lambda e, ck=ck, bk=bk: e.tensor_tensor(out=YDT[:, ck, :], in0=bk[:], in1=GZT[:, ck, :], op=ALU.mult),
                               reads=[ob, OGZT], writes=[OYD[ck]])

            def fc_O(u):
                S.emit("dve", lambda e: e.memset(ssf[:], 0.0), writes=[Ossf])
                for r in range(4):
                    xb = r % NX1R
                    S.emit("sp", lambda e, xb=xb, r=r: e.dma_start(out=x1r[xb][:], in_=rows_ap(X1P, u, r)),
                           writes=[Ox1r[xb]], dsem="x1rl%d" % xb)
                    for hf in range(2):
                        bk, ob = nb()
                        for ek in range(8):
                            S.emit("pe", lambda e, r=r, hf=hf, ek=ek, bk=bk: e.matmul(
                                out=bk[:], lhsT=YDT[:, ek, r * 128:(r + 1) * 128], rhs=Wd[:, ek, hf * 512:(hf + 1) * 512],
                                start=(ek == 0), stop=(ek == 7)), reads=[Oc, OYD[ek]], writes=[ob], inc=(ek == 7))
                        S.emit("dve", lambda e, r=r, hf=hf, xb=xb, bk=bk: e.tensor_tensor(
                            out=X2[:, r, hf * 512:(hf + 1) * 512], in0=bk[:], in1=x1r[xb][:, hf * 512:(hf + 1) * 512], op=ALU.add),
                            reads=[ob, Ox1r[xb]], writes=[OX2[r]])
                    S.emit("act", lambda e, r=r: e.activation(out=junk[:], in_=X2[:, r, :], func=AF.Square, accum_out=ssf[:, r:r + 1]),
                           reads=[OX2[r]], writes=[Ojunk, Ossf])

            def fc_O2(u):
                S.emit("dve", lambda e: e.tensor_scalar(out=ssf[:], in0=ssf[:], scalar1=1.0 / D, scalar2=EPS, op0=ALU.mult, op1=ALU.add),
                       reads=[Ossf], writes=[Ossf])
                rsqrt_dve(S, rsf[:], ssf[:], rtf[:], Orsf, Ossf, Ortf)
                for r in range(4):
                    ob_ = r
                    S.emit("dve", lambda e, r=r, ob_=ob_: e.scalar_tensor_tensor(
                        out=outs[ob_][:], in0=X2[:, r, :], scalar=rsf[:, r:r + 1], in1=gfr[:], op0=ALU.mult, op1=ALU.mult),
                        reads=[OX2[r], Orsf, Oc], writes=[Oouts[ob_]])
                    S.emit("pool", lambda e, r=r, ob_=ob_: e.dma_start(out=rows_ap(y_d, u, r), in_=outs[ob_][:]),
                           reads=[Oouts[ob_]], dsem="yst%d" % ob_)

            fc_L(0)
            fc_MCl(0)
            fc_MCc(0)
            if NU > 1:
                fc_MCl(1)
            fc_Gl(0)
            for u in range(NU):
                fc_C(u)
                if u + 1 < NU:
                    fc_L(u + 1)
                    fc_MCc(u + 1)
                if u + 2 < NU:
                    fc_MCl(u + 2)
                fc_Gt(u)
                if u + 1 < NU:
                    fc_Gl(u + 1)
                if u >= 1:
                    fc_O(u - 1)
                fc_D(u)
                if u >= 1:
                    fc_O2(u - 1)
            fc_O(NU - 1)
            fc_O2(NU - 1)
            end_phase()
    return nc


def _tables(NB, ctype):
    NT = 128 * NB
    a = np.arange(128)
    ang = 2 * np.pi * np.outer(a, a) / 128.0
    MAr = np.cos(ang)
    MAi = -np.sin(ang)
    Z = np.zeros_like(MAr)
    if ctype == 1:
        MA = np.stack([MAr, MAi, Z, Z], axis=1)
    else:
        MA = np.stack([Z, Z, MAr, MAi], axis=1)
    MA = MA.reshape(128, 512).astype(np.float32)
    k1 = np.arange(128)[None, :, None]
    if ctype == 1:
        S_ = NT
        b = np.arange(NB)[:, None, None]
        k2 = np.arange(NB)[None, None, :]
        th = 2 * np.pi * (k1 * b / S_ + k2 * b / NB)
        c = np.cos(th) / np.sqrt(S_)
        s = np.sin(th) / np.sqrt(S_)
    else:
        NB4 = NB // 4
        S_ = NT // 4
        b = np.arange(NB)[:, None, None]
        k2 = np.arange(NB)[None, None, :]
        sb_, s2 = b // NB4, b % NB4
        sk, kk2 = k2 // NB4, k2 % NB4
        th = 2 * np.pi * (k1 * s2 / S_ + kk2 * s2 / NB4)
        mask = (sb_ == sk).astype(np.float64)
        c = np.cos(th) * mask / np.sqrt(S_)
        s = np.sin(th) * mask / np.sqrt(S_)
    MC = np.zeros((2, NB, 128, 2, NB))
    MC[0, :, :, 0, :] = c
    MC[1, :, :, 0, :] = s
    MC[0, :, :, 1, :] = s
    MC[1, :, :, 1, :] = -c
    MC = MC.reshape(2 * NB, 128 * 2 * NB).astype(np.float32)
    cc = np.arange(256)
    angc = 2 * np.pi * np.outer(cc, cc) / 256.0
    DCfull = np.stack([np.cos(angc) / 16.0, -np.sin(angc) / 16.0], axis=1)
    DC = DCfull.reshape(2, 128, 2, 256).transpose(1, 0, 2, 3).reshape(128, 1024).astype(np.float32)
    inv = np.zeros((4, 4, 8))
    for g in range(4):
        w = 2 << g
        for j in range(8):
            cl = min(j + w // 2, 10 ** 9) - max(j - w // 2, 0)
            cr = min(8 - j, w // 2) + w // 2
            inv[0, g, j] = 1.0 / cl
            inv[1, g, j] = 1.0 / cr
            inv[2, g, j] = 1.0 / cl if ctype == 2 else 1.0 / w
            inv[3, g, j] = 1.0 / cr if ctype == 2 else 1.0 / w
    inv_tab = np.broadcast_to(inv.reshape(1, 128), (128, 128)).astype(np.float32).copy()
    beta = np.full((128, 1), 1.0 if ctype == 1 else 0.0, np.float32)
    ident = np.eye(128, dtype=np.float32)
    return dict(MA=MA, MC=MC, DC=DC, inv_tab=inv_tab, beta=beta, ident=ident)


def _weight_layouts(inp):
    f = np.float32
    conv_w = np.asarray(inp["ev_conv_w"][0], f)
    convw_l = conv_w.reshape(3, 8, 128).transpose(2, 1, 0).reshape(128, 24)
    pscale_l = np.asarray(inp["ev_pool_scale"][0], f).reshape(8, 128).T
    pw = np.asarray(inp["ev_pool_w"][0], f)
    poolw_l = pw.reshape(4, 2, 128, 256).transpose(2, 0, 1, 3).reshape(128, 2048)
    ws = np.asarray(inp["od_sgu_ws"][0], f)
    wsT_l = ws.transpose(2, 0, 1).reshape(128, 512)
    sgug_l = np.asarray(inp["od_sgu_norm_g"][0], f).reshape(8, 128).T
    bs = np.asarray(inp["od_sgu_bs"][0], f)
    bs_l = np.broadcast_to(bs[:, None, :], (4, 4, 128)).reshape(2048)
    fw = np.asarray(inp["od_fnet_w"][0], f)
    fnetw_l = fw.reshape(4, 2, 128, 256).transpose(2, 0, 1, 3).reshape(128, 2048)
    c = np.ascontiguousarray
    return dict(
        w_in0=c(np.asarray(inp["ev_w_in"][0], f)), w_out0=c(np.asarray(inp["ev_w_out"][0], f)),
        w_in1=c(np.asarray(inp["od_w_in"][0], f)), w_out1=c(np.asarray(inp["od_w_out"][0], f)),
        g0=c(np.asarray(inp["norm_g"][0], f)), g1=c(np.asarray(inp["norm_g"][1], f)), gf=c(np.asarray(inp["final_g"], f)),
        convw_l=c(convw_l), pscale_l=c(pscale_l), poolw_l=c(poolw_l), wsT_l=c(wsT_l), sgug_l=c(sgug_l),
        bs_l=c(bs_l), fnetw_l=c(fnetw_l))


_NC_CACHE = {}


def run(inp, debug=False, stop_after=None, lim=99):
    xp = np.asarray(inp["x_prompt"], np.float32)
    xs = np.asarray(inp["x_sample"], np.float32)
    BS, SS = xs.shape[0], xs.shape[1]
    BP, SP = xp.shape[0], xp.shape[1]
    assert BS == 4 and BP == 8 and SS == 4 * SP and SS % 512 == 0
    NT = SS
    NB = NT // 128
    key = (NB, debug, stop_after, lim)
    if key not in _NC_CACHE:
        _NC_CACHE[key] = build(NB, debug=debug, stop_after=stop_after, lim=lim)
    nc = _NC_CACHE[key]
    wl = _weight_layouts(inp)
    t1 = _tables(NB, 1)
    t2 = _tables(NB, 2)
    roles = [("s", 0), ("s", 1), ("p", 0), ("p", 1), ("s", 2), ("s", 3), ("p", 2), ("p", 3)]
    in_maps = []
    for c in range(8):
        m = dict(wl)
        kind, k = roles[c]
        if kind == "s":
            m["x"] = np.ascontiguousarray(xs[k])
            m.update(t1)
        else:
            x = np.zeros((NT, D), np.float32)
            x[0:SP] = xp[2 * k]
            x[SP:2 * SP] = xp[2 * k + 1]
            m["x"] = x
            m.update(t2)
        in_maps.append(m)
    res = run_bass_kernel_spmd(nc, in_maps, core_ids=list(range(8)))
    y_s = np.empty((4, SS, D), np.float32)
    y_p = np.empty((8, SP, D), np.float32)
    for c in range(8):
        y = res.results[c]["y"]
        kind, k = roles[c]
        if kind == "s":
            y_s[k] = y
        else:
            y_p[2 * k] = y[0:SP]
            y_p[2 * k + 1] = y[SP:2 * SP]
    return (y_p, y_s), res


def kernel(**inputs):
    out, _ = run(inputs)
    return out
```
